# Optimizing a Trainium2 kernel written in Bass

```python
import jax, jax.numpy as jnp
from jax import lax
import numpy as np

D_MODEL = 1024
BATCH = 8
SEQ = 4096
DEPTH = 4

GRID_W = 64
CTX_LEN = 256
N_MIXERS = 3
CHUNK = 128
A_WIDTH = 2 * D_MODEL
A_GROUPS = 8
B_GROUPS = 4
HEAD_DIM = 128
N_HEADS = D_MODEL // HEAD_DIM
N_KV_HEADS = 2
Q_BLOCK = 128
ROPE_THETA = 10000.0
D_FF = 4 * D_MODEL
NORM_EPS = 1e-6
LN_EPS = 1e-5
N_A = len(range(0, DEPTH, N_MIXERS))
N_B = len(range(1, DEPTH, N_MIXERS))
N_C = len(range(2, DEPTH, N_MIXERS))

kernel_name = "hybrid_gmlp_fnet_gqa_prefix_dit"


def rms_norm(x, g):
    xf = x.astype(jnp.float32)
    y = xf * lax.rsqrt(jnp.mean(xf * xf, axis=-1, keepdims=True) + NORM_EPS)
    return (y * g.astype(jnp.float32)).astype(x.dtype)


def layer_norm(x, g):
    xf = x.astype(jnp.float32)
    mu = jnp.mean(xf, axis=-1, keepdims=True)
    var = jnp.mean(jnp.square(xf - mu), axis=-1, keepdims=True)
    return ((xf - mu) * lax.rsqrt(var + LN_EPS) * g.astype(jnp.float32)).astype(x.dtype)


def ada_mod(cond, w, b):
    return jnp.split(jax.nn.silu(cond) @ w + b, 6, axis=-1)


def modulate(h, shift, scale):
    return h * (1.0 + scale) + shift


def chunk_gmlp(h, w_in, ln_g, w_s, b_s, w_out):
    bsz, length, _ = h.shape
    u, v = jnp.split(jax.nn.gelu(h @ w_in), 2, axis=-1)
    v = layer_norm(v, ln_g)
    v = v.reshape(bsz, length // CHUNK, CHUNK, A_GROUPS, A_WIDTH // A_GROUPS)
    sv = jnp.einsum('gpq,bnqgc->bnpgc', w_s, v) + b_s.T[:, :, None]
    return (u * sv.reshape(bsz, length, A_WIDTH)) @ w_out


def fourier_mix(h, w_out):
    bsz, length, d = h.shape
    hg = h.astype(jnp.float32).reshape(bsz, length, B_GROUPS, d // B_GROUPS)
    y = jnp.fft.fftn(hg, axes=(1, 3), norm="ortho").real
    return y.reshape(bsz, length, d).astype(h.dtype) @ w_out


def axial_rope(rows):
    t = jnp.arange(rows * GRID_W)
    row = (t // GRID_W).astype(jnp.float32)
    col = (t % GRID_W).astype(jnp.float32)
    n_freq = HEAD_DIM // 4
    inv = ROPE_THETA ** (-jnp.arange(n_freq, dtype=jnp.float32) / n_freq)
    ang = jnp.concatenate([row[:, None] * inv, col[:, None] * inv], axis=-1)
    return jnp.cos(ang), jnp.sin(ang)


def apply_rope(x, cos, sin):
    x1, x2 = jnp.split(x, 2, axis=-1)
    cs = cos[None, :, None, :].astype(x.dtype)
    sn = sin[None, :, None, :].astype(x.dtype)
    return jnp.concatenate([x1 * cs - x2 * sn, x1 * sn + x2 * cs], axis=-1)


def attend(q, keys, vals):
    bsz, lq = q.shape[:2]
    qg = q.reshape(bsz, lq, N_KV_HEADS, N_HEADS // N_KV_HEADS, HEAD_DIM)
    s = jnp.einsum('bqkrd,bskd->bkrqs', qg, keys).astype(jnp.float32) * (HEAD_DIM ** -0.5)
    p = jax.nn.softmax(s, axis=-1).astype(vals.dtype)
    o = jnp.einsum('bkrqs,bskd->bqkrd', p, vals)
    return o.reshape(bsz, lq, N_HEADS * HEAD_DIM)


def gqa_attention(h, hc, w_qkv, q_g, k_g, w_o, cos, sin, need_ctx_out):
    hq = N_HEADS * HEAD_DIM
    hk = N_KV_HEADS * HEAD_DIM
    w_q, w_k, w_v = w_qkv[:, :hq], w_qkv[:, hq:hq + hk], w_qkv[:, hq + hk:]

    def q_proj(z):
        bsz, length, _ = z.shape
        return rms_norm((z @ w_q).reshape(bsz, length, N_HEADS, HEAD_DIM), q_g)

    def kv_proj(z):
        bsz, length, _ = z.shape
        k = rms_norm((z @ w_k).reshape(bsz, length, N_KV_HEADS, HEAD_DIM), k_g)
        v = (z @ w_v).reshape(bsz, length, N_KV_HEADS, HEAD_DIM)
        return k, v

    q = apply_rope(q_proj(h), cos, sin)
    k, v = kv_proj(h)
    k = apply_rope(k, cos, sin)
    kc, vc = kv_proj(hc)
    keys = jnp.concatenate([kc, k], axis=1)
    vals = jnp.concatenate([vc, v], axis=1)

    bsz, length = q.shape[:2]
    nb = length // Q_BLOCK
    qb = jnp.moveaxis(q.reshape(bsz, nb, Q_BLOCK, N_HEADS, HEAD_DIM), 1, 0)
    ob = lax.map(lambda blk: attend(blk, keys, vals), qb)
    y = jnp.moveaxis(ob, 0, 1).reshape(bsz, length, hq) @ w_o

    yc = attend(q_proj(hc), kc, vc) @ w_o if need_ctx_out else None
    return y, yc


def channel_mlp(h, w1, w2):
    return jnp.square(jax.nn.relu(h @ w1)) @ w2


def setup_inputs(seed: int = 0) -> dict:
    key = jax.random.key(seed)
    ks = jax.random.split(key, 20)
    f32 = jnp.float32
    D = D_MODEL

    def nrm(k, shape, scale):
        return jax.random.normal(k, shape, f32) * scale

    return {
        "x": nrm(ks[0], (BATCH, SEQ, D), 1.0),
        "c": nrm(ks[1], (BATCH, D), 1.0),
        "ctx": nrm(ks[2], (BATCH, CTX_LEN, D), 1.0),
        "c_ctx": nrm(ks[3], (D,), 1.0),
        "ada_w": nrm(ks[4], (DEPTH, D, 6 * D), 0.5 * D ** -0.5),
        "ada_b": nrm(ks[5], (DEPTH, 6 * D), 0.02),
        "norm_g": 1.0 + nrm(ks[6], (DEPTH, 4, D), 0.02),
        "mlp_w1": nrm(ks[7], (DEPTH, D, D_FF), D ** -0.5),
        "mlp_w2": nrm(ks[8], (DEPTH, D_FF, D), D_FF ** -0.5),
        "a_w_in": nrm(ks[9], (N_A, D, 2 * A_WIDTH), D ** -0.5),
        "a_ln_g": 1.0 + nrm(ks[10], (N_A, A_WIDTH), 0.02),
        "a_w_s": nrm(ks[11], (N_A, A_GROUPS, CHUNK, CHUNK), CHUNK ** -0.5),
        "a_b_s": 1.0 + nrm(ks[12], (N_A, A_GROUPS, CHUNK), 0.02),
        "a_w_out": nrm(ks[13], (N_A, A_WIDTH, D), A_WIDTH ** -0.5),
        "b_w_out": nrm(ks[14], (N_B, D, D), D ** -0.5),
        "c_w_qkv": nrm(ks[15], (N_C, D, (N_HEADS + 2 * N_KV_HEADS) * HEAD_DIM), D ** -0.5),
        "c_q_g": 1.0 + nrm(ks[16], (N_C, HEAD_DIM), 0.02),
        "c_k_g": 1.0 + nrm(ks[17], (N_C, HEAD_DIM), 0.02),
        "c_w_o": nrm(ks[18], (N_C, N_HEADS * HEAD_DIM, D), (N_HEADS * HEAD_DIM) ** -0.5),
    }


def reference(x, c, ctx, c_ctx, ada_w, ada_b, norm_g, mlp_w1, mlp_w2, a_w_in, a_ln_g, a_w_s, a_b_s, a_w_out,
              b_w_out, c_w_qkv, c_q_g, c_k_g, c_w_o):
    rows = x.shape[1] // GRID_W
    cos, sin = axial_rope(rows)
    attn_layers = [i for i in range(DEPTH) if i % N_MIXERS == 2]
    last_ctx_read = attn_layers[-1] if attn_layers else -1

    for i in range(DEPTH):
        kind, j = i % N_MIXERS, i // N_MIXERS
        ctx_in = i <= last_ctx_read
        ctx_out = i < last_ctx_read

        sh1, sc1, g1, sh2, sc2, g2 = [m[:, None, :] for m in ada_mod(c, ada_w[i], ada_b[i])]
        h = modulate(rms_norm(x, norm_g[i, 0]), sh1, sc1)
        hc = None
        if ctx_in:
            csh1, csc1, cg1, csh2, csc2, cg2 = ada_mod(c_ctx, ada_w[i], ada_b[i])
            hc = modulate(rms_norm(ctx, norm_g[i, 0]), csh1, csc1)

        if kind == 0:
            y = chunk_gmlp(h, a_w_in[j], a_ln_g[j], a_w_s[j], a_b_s[j], a_w_out[j])
            yc = chunk_gmlp(hc, a_w_in[j], a_ln_g[j], a_w_s[j], a_b_s[j], a_w_out[j]) if ctx_out else None
        elif kind == 1:
            y = fourier_mix(h, b_w_out[j])
            yc = fourier_mix(hc, b_w_out[j]) if ctx_out else None
        else:
            y, yc = gqa_attention(h, hc, c_w_qkv[j], c_q_g[j], c_k_g[j], c_w_o[j], cos, sin, ctx_out)

        x = x + g1 * rms_norm(y, norm_g[i, 1])
        hm = modulate(rms_norm(x, norm_g[i, 2]), sh2, sc2)
        x = x + g2 * rms_norm(channel_mlp(hm, mlp_w1[i], mlp_w2[i]), norm_g[i, 3])

        if ctx_out:
            ctx = ctx + cg1 * rms_norm(yc, norm_g[i, 1])
            hcm = modulate(rms_norm(ctx, norm_g[i, 2]), csh2, csc2)
            ctx = ctx + cg2 * rms_norm(channel_mlp(hcm, mlp_w1[i], mlp_w2[i]), norm_g[i, 3])

    return x
```

```python
from contextlib import ExitStack
import numpy as np
import ml_dtypes
import concourse.bass as bass
import concourse.mybir as mybir
from concourse.bass_utils import run_bass_kernel_spmd

F32 = mybir.dt.float32
BF16 = mybir.dt.bfloat16
AF = mybir.ActivationFunctionType
ALU = mybir.AluOpType

D = 1024
SEQ = 4096
CTX = 256
NG = 8
GT = 512
NKEY = CTX + SEQ
NSC = NKEY // 128
QKVW = 2816


class T:
    __slots__ = ("name", "w", "r", "dsem", "dcnt")

    def __init__(self, name, dsem=None):
        self.name = name
        self.w = {}
        self.r = {}
        self.dsem = dsem
        self.dcnt = 0


class Sched:
    QS = ("pe", "act", "dve", "pool", "sp")

    def __init__(self, nc, stack):
        self.nc = nc
        self.stack = stack
        self.prog = {q: [] for q in self.QS}
        self.semh = {}
        self.cnt = {}
        for q in ("pe", "act", "dve", "pool"):
            self.semh[q] = stack.enter_context(nc.semaphore("s_" + q))
            self.cnt[q] = 0
        self.seen = {q: {} for q in self.QS}
        self.dma_objs = []

    def tracked(self, name, dma=False):
        t = T(name)
        if dma:
            key = "d_" + name
            self.semh[key] = self.stack.enter_context(self.nc.semaphore(key))
            t.dsem = key
            self.dma_objs.append(t)
        return t

    def _waits(self, q, reads, writes):
        need = {}
        for t in reads:
            for s, v in t.w.items():
                if need.get(s, 0) < v:
                    need[s] = v
        for t in writes:
            for s, v in t.w.items():
                if need.get(s, 0) < v:
                    need[s] = v
            for s, v in t.r.items():
                if need.get(s, 0) < v:
                    need[s] = v
        out = []
        seen = self.seen[q]
        for s, v in need.items():
            if q == "pe" and s == "pe":
                continue
            if seen.get(s, 0) < v:
                seen[s] = v
                out.append((s, v))
        return out

    def _reg(self, ev, reads, writes):
        s, v = ev
        for t in reads:
            if t.r.get(s, 0) < v:
                t.r[s] = v
        for t in writes:
            t.w = {s: v}
            t.r = {}

    def op(self, q, fn, reads=(), writes=(), inc=True):
        waits = self._waits(q, reads, writes)
        v = self.cnt[q] + 1
        if inc:
            self.cnt[q] = v
        self.prog[q].append((waits, fn, (q, 1) if inc else None))
        self._reg((q, v), reads, writes)

    def dma(self, q, fns, reads, writes, semobj):
        waits = self._waits(q, reads, writes)
        semobj.dcnt += len(fns)
        ev = (semobj.dsem, 16 * semobj.dcnt)
        for i, fn in enumerate(fns):
            self.prog[q].append((waits if i == 0 else [], fn, (semobj.dsem, 16)))
        self._reg(ev, reads, writes)

    def final_wait(self, q="sp"):
        waits = []
        for t in self.dma_objs:
            if t.dcnt:
                waits.append((t.dsem, 16 * t.dcnt))
        for e in ("pe", "act", "dve", "pool"):
            if self.cnt[e]:
                waits.append((e, self.cnt[e]))
        self.prog[q].append((waits, None, None))

    def emit(self):
        nc = self.nc
        semh = self.semh
        prog = self.prog

        def replay(q, eng):
            for waits, fn, inc in prog[q]:
                for s, v in waits:
                    eng.wait_ge(semh[s], v)
                if fn is None:
                    continue
                ins = fn(eng)
                if inc is not None:
                    ins.then_inc(semh[inc[0]], inc[1])

        with nc.Block() as block:
            @block.tensor
            def _(e):
                replay("pe", e)

            @block.scalar
            def _(e):
                replay("act", e)

            @block.vector
            def _(e):
                replay("dve", e)

            @block.gpsimd
            def _(e):
                replay("pool", e)

            @block.sync
            def _(e):
                replay("sp", e)


def build_program(debug=False):
    nc = bass.Bass("TRN2", target_bir_lowering=False)

    def din(name, shape, dt=F32):
        return nc.dram_tensor(name, list(shape), dt, kind="ExternalInput").ap()

    def dint(name, shape, dt):
        return nc.dram_tensor(name, list(shape), dt, kind="Internal").ap()

    xT_d = din("xT", [D, SEQ])
    ctxT_d = din("ctxT", [D, CTX])
    cc_d = din("cc", [128, 8, 2])
    adaw_d = din("ada_w", [4, D, 6 * D])
    adab_d = din("ada_b", [128, 4, 48])
    ng_d = din("norm_g", [128, 4, 4, 8])
    w1_d = din("mlp_w1", [4, D, 4 * D])
    w2_d = din("mlp_w2", [4, 4 * D, D])
    awin_d = din("a_w_in", [2, D, 4 * D])
    alng_d = din("a_ln_g", [128, 2, 16])
    aws_d = din("a_w_s", [2, 128, 8, 128])
    abs_d = din("a_b_s", [2, 128, 8, 128])
    awout_d = din("a_w_out", [2, 2 * D, D])
    bwout_d = din("b_w_out", [D, D])
    qkv_d = din("c_w_qkv", [D, QKVW])
    qkg_d = din("c_qk_g", [128, 4])
    wo_d = din("c_w_o", [D, D])
    dftc_d = din("dft_c", [128, 2, 512], BF16)
    dftl_d = din("dft_l", [8, 32, 128, 2, 512], BF16)
    dctx_d = din("dft_ctx", [128, 2, 2, 256], BF16)
    rope_d = din("rope", [2, 128, SEQ])
    out_d = nc.dram_tensor("outT", [D, SEQ], F32, kind="ExternalOutput").ap()
    if debug:
        dbg_d = nc.dram_tensor("dbg", [3, D, SEQ + CTX], F32, kind="ExternalOutput").ap()

    w1_b = dint("w1_b", [4, D, 4 * D], BF16)
    w2_b = dint("w2_b", [4, 4 * D, D], BF16)
    awin_b = dint("awin_b", [2, D, 4 * D], BF16)
    awout_b = dint("awout_b", [2, 2 * D, D], BF16)
    bwout_b = dint("bwout_b", [D, D], BF16)
    qkv_b = dint("qkv_b", [D, QKVW], BF16)
    wo_b = dint("wo_b", [D, D], BF16)
    xs_d = dint("xs", [D, SEQ + CTX], F32)
    pq_d = dint("pq", [34, 128, 2048], BF16)
    qT_dd = dint("qTs", [NG, 128, 8, GT], BF16)

    with ExitStack() as st:
        S = Sched(nc, st)

        def sb(name, shape, dt):
            return st.enter_context(nc.sbuf_tensor("sb_" + name, list(shape), dt))

        xTs = [sb(f"xT{i}", [128, 8, GT], F32) for i in range(2)]
        Xs = [[S.tracked(f"x{i}_{c}") for c in range(8)] for i in range(2)]
        Xds = [S.tracked(f"xdma{i}", dma=True) for i in range(2)]
        CX = {"i": 0}
        hT = sb("hT", [128, 8, GT], BF16)
        H = [S.tracked(f"h{c}") for c in range(8)]
        Hd = S.tracked("hdma", dma=True)
        yT = sb("yT", [128, 8, GT], F32)
        Y = [S.tracked(f"y{c}") for c in range(8)]
        hid = sb("hid", [128, 32, GT], BF16)
        HID = [S.tracked(f"hid{c}") for c in range(32)]
        NSQ = 3
        sq = [sb(f"sq{i}", [128, GT], BF16) for i in range(NSQ)]
        SQ = [S.tracked(f"sq{i}") for i in range(NSQ)]
        NTMP = 4
        tmp = [sb(f"tmp{i}", [128, GT], F32) for i in range(NTMP)]
        TMP = [S.tracked(f"tmp{i}") for i in range(NTMP)]
        rst = [sb(f"rst{i}", [128, GT], F32) for i in range(2)]
        RST = [S.tracked(f"rst{i}") for i in range(2)]
        NW = 3
        pan = [sb(f"pan{i}", [128, 8, 512], BF16) for i in range(NW)]
        PAN = [S.tracked(f"pan{i}", dma=True) for i in range(NW)]
        ov = sb("ov", [128, 8192], BF16)
        vt = ov[:, :].rearrange("p (t f) -> p t f", t=4)
        VT = [S.tracked(f"vt{t}") for t in range(4)]
        st1 = sb("st1", [128, 4, 4], F32)
        st2 = sb("st2", [128, 4, 4], F32)
        stm = sb("stm", [128, 8, 4], F32)
        ST = S.tracked("st")
        KT = sb("KT", [128, 2, NKEY], BF16)
        KTt = S.tracked("KT")
        Vs = sb("Vs", [128, NSC, 256], BF16)
        Vst = S.tracked("Vs")
        NPT = 3
        pt = [sb(f"pt{i}", [128, GT], BF16) for i in range(NPT)]
        PT = [S.tracked(f"pt{i}") for i in range(NPT)]
        ropet = sb("ropet", [128, 2, GT], F32)
        ROPE = S.tracked("rope", dma=True)
        NPQ = 2
        pqt = [sb(f"pqt{i}", [128, 2048], BF16) for i in range(NPQ)]
        PQT = [S.tracked(f"pqt{i}", dma=True) for i in range(NPQ)]
        NCS = 6
        cst = [ov[:, i * 1024:(i + 1) * 1024].rearrange("p (a k) -> p a k", a=2) for i in range(NCS)]
        CST = [S.tracked(f"cst{i}", dma=True) for i in range(NCS)]
        ones = sb("ones", [128, 128], BF16)
        cvals = sb("cvals", [128, 4], F32)
        CONST = S.tracked("const")
        cc = sb("cc", [128, 8, 2], F32)
        ccb = sb("ccb", [128, 8, 2], F32)
        CC = S.tracked("cc", dma=True)
        CCB = S.tracked("ccb")
        adab = sb("adab", [128, 4, 48], F32)
        ngt = sb("ngt", [128, 4, 4, 8], F32)
        alng = sb("alng", [128, 2, 16], F32)
        qkg = sb("qkg", [128, 4], F32)
        SMALL = S.tracked("small", dma=True)
        dftc = sb("dftc", [128, 2, 512], BF16)
        dctx = sb("dctx", [128, 2, 2, 256], BF16)
        TAB = S.tracked("tab", dma=True)
        wsb = sb("wsb", [128, 8, 128], BF16)
        bsb = sb("bsb", [128, 8, 128], F32)
        WSF = S.tracked("wsf", dma=True)
        WSB = S.tracked("wsb")
        mods = sb("mods", [128, 4, 2, 48], F32)
        MODS = [S.tracked(f"mods{l}") for l in range(4)]
        dsc = sb("dsc", [128, 4, 2, 4, 8], F32)
        DSC = [S.tracked(f"dsc{l}") for l in range(4)]

        ps = [st.enter_context(nc.psum_tensor(f"ps{i}", [128, 512], F32)) for i in range(8)]
        PS = [S.tracked(f"ps{i}") for i in range(8)]
        rr = {"ps": 0, "pan": 0, "sq": 0, "tmp": 0, "rst": 0, "pt": 0, "pqt": 0, "cst": 0}

        def nxt(key, n):
            i = rr[key]
            rr[key] = (i + 1) % n
            return i

        def bank():
            i = nxt("ps", 8)
            return ps[i], PS[i]

        W_W1 = [S.tracked(f"Ww1{l}", dma=True) for l in range(4)]
        W_W2 = [S.tracked(f"Ww2{l}", dma=True) for l in range(4)]
        W_AIN = [S.tracked(f"Wain{j}", dma=True) for j in range(2)]
        W_AOUT = [S.tracked(f"Waout{j}", dma=True) for j in range(2)]
        W_BOUT = S.tracked("Wbout", dma=True)
        W_QKV = S.tracked("Wqkv", dma=True)
        W_WO = S.tracked("Wwo", dma=True)
        XS = [S.tracked(f"XS{g}") for g in range(NG + 1)]
        PQD = [S.tracked(f"PQD{g}") for g in range(NG + 1)]
        QTD = [S.tracked(f"QTD{g}") for g in range(NG)]
        OUTD = S.tracked("outd")
        STOs = [S.tracked(f"sto{i}", dma=True) for i in range(2)]
        PQS = [S.tracked(f"pqs{i}", dma=True) for i in range(NPQ)]
        STQ = S.tracked("stq", dma=True)

        def conv(dst, src, rows, cols, obj, maxel=1 << 20):
            step = max(1, maxel // cols)
            fns = []
            for r0 in range(0, rows, step):
                r1 = min(rows, r0 + step)
                fns.append(lambda e, r0=r0, r1=r1: e.dma_start(out=dst[r0:r1, :], in_=src[r0:r1, :]))
            S.dma("pool", fns, [], [obj], obj)

        S.op("pool", lambda e: e.memset(ones[:], 1.0), [], [CONST])
        S.op("pool", lambda e: e.memset(cvals[:, 0:1], 1e-6), [], [CONST])
        S.op("pool", lambda e: e.memset(cvals[:, 1:2], 1e-5), [], [CONST])
        S.op("pool", lambda e: e.memset(cvals[:, 2:3], -11.313708498984761), [], [CONST])
        S.dma("sp", [lambda e: e.dma_start(out=cc[:], in_=cc_d[:, :, :])], [], [CC], CC)
        S.dma("sp", [lambda e: e.dma_start(out=adab[:], in_=adab_d[:, :, :]),
                     lambda e: e.dma_start(out=ngt[:], in_=ng_d[:, :, :, :]),
                     lambda e: e.dma_start(out=alng[:], in_=alng_d[:, :, :]),
                     lambda e: e.dma_start(out=qkg[:], in_=qkg_d[:, :])], [], [SMALL], SMALL)
        S.dma("sp", [lambda e: e.dma_start(out=dftc[:], in_=dftc_d[:, :, :]),
                     lambda e: e.dma_start(out=dctx[:], in_=dctx_d[:, :, :, :])], [], [TAB], TAB)

        conv(awin_b[0], awin_d[0], D, 4 * D, W_AIN[0])
        conv(awout_b[0], awout_d[0], 2 * D, D, W_AOUT[0])
        conv(w1_b[0], w1_d[0], D, 4 * D, W_W1[0])
        conv(w2_b[0], w2_d[0], 4 * D, D, W_W2[0])
        conv(bwout_b, bwout_d, D, D, W_BOUT)
        conv(w1_b[1], w1_d[1], D, 4 * D, W_W1[1])
        conv(w2_b[1], w2_d[1], 4 * D, D, W_W2[1])
        conv(qkv_b, qkv_d, D, QKVW, W_QKV)
        conv(wo_b, wo_d, D, D, W_WO)
        conv(w1_b[2], w1_d[2], D, 4 * D, W_W1[2])
        conv(w2_b[2], w2_d[2], 4 * D, D, W_W2[2])
        conv(awin_b[1], awin_d[1], D, 4 * D, W_AIN[1])
        conv(awout_b[1], awout_d[1], 2 * D, D, W_AOUT[1])
        conv(w1_b[3], w1_d[3], D, 4 * D, W_W1[3])
        conv(w2_b[3], w2_d[3], 4 * D, D, W_W2[3])

        def load_panel(wobj, wap, r0, c0, kc=8, ncols=512, filler=True):
            if filler:
                filler_step()
            i = nxt("pan", NW)
            src = wap[r0:r0 + kc * 128, c0:c0 + ncols].rearrange("(kc p) n -> p kc n", p=128)
            S.dma("sp", [lambda e: e.dma_start(out=pan[i][:, 0:kc, 0:ncols], in_=src)], [wobj], [PAN[i]], PAN[i])
            return pan[i], PAN[i]

        S.op("act", lambda e: e.activation(out=ccb[:], in_=cc[:], func=AF.Silu), [CC], [CCB])
        modprog = [0, 0, 0, 0]

        def mods_gen(l):
            derived = {1: (0, 0, False), 2: (1, 1, True), 4: (2, 2, False), 5: (3, 3, True)}
            for pn in range(24):
                i = nxt("pan", NW)
                pf = pan[i][:, :, :].bitcast(F32)
                src = adaw_d[l][:, pn * 256:(pn + 1) * 256].rearrange("(kc p) n -> p kc n", p=128)
                S.dma("sp", [lambda e, pf=pf, src=src: e.dma_start(out=pf, in_=src)], [], [PAN[i]], PAN[i])
                P_ = PAN[i]
                bk, BK = bank()
                for jj in range(2):
                    for kc in range(8):
                        S.op("pe", lambda e, bk=bk, pf=pf, jj=jj, kc=kc: e.matmul(
                            bk[:, 2 * jj:2 * jj + 2], lhsT=pf[:, kc, jj * 128:(jj + 1) * 128], rhs=ccb[:, kc, :],
                            start=(kc == 0), stop=(kc == 7)), [P_, CCB], [BK], inc=(kc == 7 and jj == 1))
                bv = bk[:, 0:4].rearrange("p (j t) -> p j t", t=2)
                for col in range(2):
                    S.op("dve", lambda e, bv=bv, l=l, col=col, pn=pn: e.tensor_tensor(
                        out=mods[:, l, col, pn * 2:(pn + 1) * 2], in0=bv[:, :, col], in1=adab[:, l, pn * 2:(pn + 1) * 2],
                        op=ALU.add), [BK, SMALL], [MODS[l]])
                if pn % 4 == 3:
                    m = pn // 4
                    if m in derived:
                        k, gi, is_gate = derived[m]
                        for col in range(2):
                            src2 = mods[:, l, col, m * 8:(m + 1) * 8]
                            if is_gate:
                                S.op("dve", lambda e, src2=src2, l=l, col=col, k=k, gi=gi: e.tensor_tensor(
                                    out=dsc[:, l, col, k, :], in0=src2, in1=ngt[:, l, gi, :], op=ALU.mult),
                                    [MODS[l], SMALL], [DSC[l]])
                            else:
                                S.op("dve", lambda e, src2=src2, l=l, col=col, k=k, gi=gi: e.scalar_tensor_tensor(
                                    out=dsc[:, l, col, k, :], in0=src2, scalar=1.0, in1=ngt[:, l, gi, :],
                                    op0=ALU.add, op1=ALU.mult), [MODS[l], SMALL], [DSC[l]])
                    modprog[l] = m + 1
                yield

        gens = [mods_gen(l) for l in range(4)]
        fill = {"k": 0}

        def advance(l):
            try:
                next(gens[l])
            except StopIteration:
                pass

        def filler_step():
            fill["k"] += 1
            for l in range(4):
                if modprog[l] < 6:
                    if l <= 1 or fill["k"] % 3 == 0:
                        advance(l)
                    return

        def need_mods(l, m):
            while modprog[l] <= m:
                advance(l)

        def sc_A1(l, col, c): return dsc[:, l, col, 0, c:c + 1]
        def sc_G1(l, col, c): return dsc[:, l, col, 1, c:c + 1]
        def sc_A2(l, col, c): return dsc[:, l, col, 2, c:c + 1]
        def sc_G2(l, col, c): return dsc[:, l, col, 3, c:c + 1]
        def sc_B1(l, col, c): return mods[:, l, col, 0 * 8 + c:0 * 8 + c + 1]
        def sc_B2(l, col, c): return mods[:, l, col, 3 * 8 + c:3 * 8 + c + 1]

        def rms_rstd(srcs, SRCS, n, nfeat, eps_col=0):
            bk, BK = bank()
            nchunk = len(srcs)
            for c in range(nchunk):
                i = nxt("sq", NSQ)
                S.op("act", lambda e, i=i, c=c: e.activation(out=sq[i][:, :n], in_=srcs[c], func=AF.Square),
                     [SRCS[c]], [SQ[i]])
                S.op("pe", lambda e, i=i, c=c, bk=bk: e.matmul(bk[:, :n], lhsT=ones[:, :], rhs=sq[i][:, :n],
                                                             start=(c == 0), stop=(c == nchunk - 1)),
                     [SQ[i], CONST], [BK], inc=True)
            r = nxt("rst", 2)
            S.op("act", lambda e, r=r, bk=bk: e.activation(out=rst[r][:, :n], in_=bk[:, :n], func=AF.Sqrt,
                                                          bias=cvals[:, eps_col:eps_col + 1], scale=1.0 / nfeat),
                 [BK, CONST], [RST[r]])
            S.op("dve", lambda e, r=r: e.reciprocal(out=rst[r][:, :n], in_=rst[r][:, :n]), [RST[r]], [RST[r]])
            return rst[r], RST[r]

        def x_stats(n, xb):
            xT, X = xTs[xb], Xs[xb]
            return rms_rstd([xT[:, c, :n] for c in range(8)], X, n, D)

        def pre_norm(l, col, n, which, pre=None):
            need_mods(l, 1 if which == 1 else 4)
            xT, X = xTs[CX["i"]], Xs[CX["i"]]
            R, RT = pre if pre is not None else x_stats(n, CX["i"])
            for c in range(8):
                i = nxt("tmp", NTMP)
                S.op("dve", lambda e, i=i, c=c: e.tensor_tensor(out=tmp[i][:, :n], in0=xT[:, c, :n], in1=R[:, :n],
                                                              op=ALU.mult), [X[c], RT], [TMP[i]])
                a = sc_A1(l, col, c) if which == 1 else sc_A2(l, col, c)
                b = sc_B1(l, col, c) if which == 1 else sc_B2(l, col, c)
                S.op("act", lambda e, i=i, c=c, a=a, b=b: e.activation(out=hT[:, c, :n], in_=tmp[i][:, :n],
                                                                      func=AF.Identity, bias=b, scale=a),
                     [TMP[i], DSC[l], MODS[l]], [H[c]])

        def post_norm(l, col, n, which):
            need_mods(l, 2 if which == 1 else 5)
            xT, X = xTs[CX["i"]], Xs[CX["i"]]
            R, RT = rms_rstd([yT[:, c, :n] for c in range(8)], Y, n, D)
            for c in range(8):
                i = nxt("tmp", NTMP)
                S.op("dve", lambda e, i=i, c=c: e.tensor_tensor(out=tmp[i][:, :n], in0=yT[:, c, :n], in1=R[:, :n],
                                                              op=ALU.mult), [Y[c], RT], [TMP[i]])
                g = sc_G1(l, col, c) if which == 1 else sc_G2(l, col, c)
                S.op("dve", lambda e, i=i, c=c, g=g: e.scalar_tensor_tensor(
                    out=xT[:, c, :n], in0=tmp[i][:, :n], scalar=g, in1=xT[:, c, :n], op0=ALU.mult, op1=ALU.add),
                    [TMP[i], DSC[l], X[c]], [X[c]])

        def proj_to_y(wobj, wap, kchunks, rhs_fn, RHS, n):
            nkb = kchunks // 8
            for cp in range(2):
                bks = [bank() for _ in range(4)]
                for kb in range(nkb):
                    p_, P_ = load_panel(wobj, wap, kb * 1024, cp * 512)
                    for jj in range(4):
                        bk, BK = bks[jj]
                        for kc in range(8):
                            kk = kb * 8 + kc
                            last = (kb == nkb - 1 and kc == 7)
                            S.op("pe", lambda e, bk=bk, p_=p_, jj=jj, kc=kc, kk=kk, last=last: e.matmul(
                                bk[:, :n], lhsT=p_[:, kc, jj * 128:(jj + 1) * 128], rhs=rhs_fn(kk),
                                start=(kk == 0), stop=last), [P_, RHS[kk]], [BK], inc=(kc == 7))
                for jj in range(4):
                    j = cp * 4 + jj
                    bk, BK = bks[jj]
                    if jj % 2 == 0:
                        S.op("dve", lambda e, bk=bk, j=j: e.tensor_copy(out=yT[:, j, :n], in_=bk[:, :n]), [BK], [Y[j]])
                    else:
                        S.op("act", lambda e, bk=bk, j=j: e.activation(out=yT[:, j, :n], in_=bk[:, :n], func=AF.Copy),
                             [BK], [Y[j]])

        def mlp(l, col, n, hook=None):
            pre_norm(l, col, n, 2)
            for pn in range(8):
                p_, P_ = load_panel(W_W1[l], w1_b[l], 0, pn * 512)
                for jj in range(4):
                    hc = pn * 4 + jj
                    bk, BK = bank()
                    for kc in range(8):
                        S.op("pe", lambda e, bk=bk, p_=p_, jj=jj, kc=kc: e.matmul(
                            bk[:, :n], lhsT=p_[:, kc, jj * 128:(jj + 1) * 128], rhs=hT[:, kc, :n],
                            start=(kc == 0), stop=(kc == 7)), [P_, H[kc]], [BK], inc=(kc == 7))
                    i = nxt("tmp", NTMP)
                    S.op("act", lambda e, bk=bk, i=i: e.activation(out=tmp[i][:, :n], in_=bk[:, :n], func=AF.Relu),
                         [BK], [TMP[i]])
                    S.op("dve", lambda e, i=i, hc=hc: e.tensor_tensor(out=hid[:, hc, :n], in0=tmp[i][:, :n],
                                                                   in1=tmp[i][:, :n], op=ALU.mult),
                         [TMP[i]], [HID[hc]])
            proj_to_y(W_W2[l], w2_b[l], 32, lambda kk: hid[:, kk, :n], HID, n)
            if hook is not None:
                hook()
            post_norm(l, col, n, 2)

        def gmlp(l, j, col, n):
            Tn = n // 128
            for pn in range(4):
                p_, P_ = load_panel(W_AIN[j], awin_b[j], 0, pn * 512)
                for jj in range(4):
                    fc = pn * 4 + jj
                    bk, BK = bank()
                    for kc in range(8):
                        S.op("pe", lambda e, bk=bk, p_=p_, jj=jj, kc=kc: e.matmul(
                            bk[:, :n], lhsT=p_[:, kc, jj * 128:(jj + 1) * 128], rhs=hT[:, kc, :n],
                            start=(kc == 0), stop=(kc == 7)), [P_, H[kc]], [BK], inc=(kc == 7))
                    S.op("act", lambda e, bk=bk, fc=fc: e.activation(out=hid[:, fc, :n], in_=bk[:, :n],
                                                                    func=AF.Gelu_apprx_tanh), [BK], [HID[fc]])
            S.op("dve", lambda e: e.memset(st1[:], 0.0), [], [ST])
            S.op("dve", lambda e: e.memset(st2[:], 0.0), [], [ST])
            for pv in range(4):
                p_, P_ = load_panel(W_AIN[j], awin_b[j], 0, 2048 + pv * 512)
                for t in range(Tn):
                    bk, BK = bank()
                    for kc in range(8):
                        S.op("pe", lambda e, bk=bk, p_=p_, t=t, kc=kc: e.matmul(
                            bk[:, :], lhsT=hT[:, kc, t * 128:(t + 1) * 128], rhs=p_[:, kc, :],
                            start=(kc == 0), stop=(kc == 7)), [P_, H[kc]], [BK], inc=(kc == 7))
                    S.op("act", lambda e, bk=bk, t=t, pv=pv: e.activation(
                        out=vt[:, t, pv * 512:(pv + 1) * 512], in_=bk[:, :], func=AF.Gelu_apprx_tanh,
                        accum_out=st1[:, t, pv:pv + 1]), [BK, ST], [VT[t], ST] + CST)
                    i = nxt("sq", NSQ)
                    S.op("act", lambda e, i=i, t=t, pv=pv: e.activation(
                        out=sq[i][:, :], in_=vt[:, t, pv * 512:(pv + 1) * 512], func=AF.Square,
                        accum_out=st2[:, t, pv:pv + 1]), [VT[t], ST], [SQ[i], ST])
            S.op("dve", lambda e: e.tensor_reduce(out=stm[:, 0, :], in_=st1[:], axis=mybir.AxisListType.X, op=ALU.add),
                 [ST], [ST])
            S.op("dve", lambda e: e.tensor_reduce(out=stm[:, 1, :], in_=st2[:], axis=mybir.AxisListType.X, op=ALU.add),
                 [ST], [ST])
            S.op("dve", lambda e: e.tensor_scalar(out=stm[:, 0, :], in0=stm[:, 0, :], scalar1=1.0 / 2048, scalar2=None,
                                                  op0=ALU.mult), [ST], [ST])
            S.op("dve", lambda e: e.tensor_scalar(out=stm[:, 1, :], in0=stm[:, 1, :], scalar1=1.0 / 2048, scalar2=None,
                                                  op0=ALU.mult), [ST], [ST])
            S.op("dve", lambda e: e.tensor_tensor(out=stm[:, 3, :], in0=stm[:, 0, :], in1=stm[:, 0, :], op=ALU.mult),
                 [ST], [ST])
            S.op("dve", lambda e: e.tensor_tensor(out=stm[:, 1, :], in0=stm[:, 1, :], in1=stm[:, 3, :], op=ALU.subtract),
                 [ST], [ST])
            S.op("act", lambda e: e.activation(out=stm[:, 2, :], in_=stm[:, 1, :], func=AF.Sqrt, bias=cvals[:, 1:2],
                                               scale=1.0), [ST, CONST], [ST])
            S.op("dve", lambda e: e.reciprocal(out=stm[:, 2, :], in_=stm[:, 2, :]), [ST], [ST])
            for t in range(Tn):
                S.op("dve", lambda e, t=t: e.tensor_scalar(out=vt[:, t, :], in0=vt[:, t, :], scalar1=stm[:, 0, t:t + 1],
                                                           scalar2=stm[:, 2, t:t + 1], op0=ALU.subtract, op1=ALU.mult),
                     [VT[t], ST], [VT[t]])
            for fc in range(16):
                g = fc // 2
                bk, BK = bank()
                for t in range(Tn):
                    S.op("pe", lambda e, bk=bk, t=t, fc=fc, g=g: e.matmul(
                        bk[:, t * 128:(t + 1) * 128], lhsT=vt[:, t, fc * 128:(fc + 1) * 128], rhs=wsb[:, g, :],
                        start=True, stop=True), [VT[t], WSB], [BK], inc=(t == Tn - 1))
                i = nxt("tmp", NTMP)
                S.op("dve", lambda e, bk=bk, i=i, fc=fc, g=g: e.scalar_tensor_tensor(
                    out=tmp[i][:, :n].rearrange("p (t q) -> p t q", q=128),
                    in0=bk[:, :n].rearrange("p (t q) -> p t q", q=128), scalar=alng[:, j, fc:fc + 1],
                    in1=bsb[:, g, :].unsqueeze(1).broadcast_to([128, Tn, 128]), op0=ALU.mult, op1=ALU.add),
                    [BK, SMALL, WSF], [TMP[i]])
                S.op("dve", lambda e, i=i, fc=fc: e.tensor_tensor(out=hid[:, 16 + fc, :n], in0=tmp[i][:, :n],
                                                                  in1=hid[:, fc, :n], op=ALU.mult),
                     [TMP[i], HID[fc]], [HID[16 + fc]])
            proj_to_y(W_AOUT[j], awout_b[j], 16, lambda kk: hid[:, 16 + kk, :n], HID[16:], n)

        def load_ws(j):
            wsf = yT[:, 0:2, :].rearrange("p a (g q) -> p (a g) q", q=128)
            S.dma("sp", [lambda e: e.dma_start(out=wsf, in_=aws_d[j]),
                         lambda e: e.dma_start(out=bsb[:], in_=abs_d[j])], [], [WSF, Y[0], Y[1]], WSF)
            S.op("pool", lambda e: e.tensor_copy(out=wsb[:], in_=wsf), [WSF, Y[0], Y[1]], [WSB])

        def load_x(src_ap, t0, n, RD, xb):
            xT = xTs[xb]
            S.dma("sp", [lambda e: e.dma_start(out=xT[:, :, :n],
                                               in_=src_ap[:, t0:t0 + n].rearrange("(c p) t -> p c t", p=128))],
                  RD, Xs[xb], Xds[xb])

        def store_x(dst_ap, t0, n, WR):
            xb = CX["i"]
            xT = xTs[xb]
            S.dma("pool", [lambda e: e.dma_start(out=dst_ap[:, t0:t0 + n].rearrange("(c p) t -> p c t", p=128),
                                                 in_=xT[:, :, :n])], Xs[xb], WR, STOs[xb])

        PRE = {"loaded": False, "stats": None}

        def begin_group(src_ap, t0, n, RD):
            if PRE["loaded"]:
                CX["i"] = 1 - CX["i"]
                PRE["loaded"] = False
            else:
                load_x(src_ap, t0, n, RD, CX["i"])
                PRE["stats"] = None
            st_ = PRE["stats"]
            PRE["stats"] = None
            return st_

        def prefetch_load(src_ap, t0, n, RD):
            load_x(src_ap, t0, n, RD, 1 - CX["i"])
            PRE["loaded"] = True

        def prefetch_stats(n):
            PRE["stats"] = x_stats(n, 1 - CX["i"])

        groups = [(g, g * GT, GT, 0) for g in range(NG)] + [(NG, SEQ, CTX, 1)]

        load_ws(0)
        def sweep1_group(g, t0, n, col, nx):
            Tn = n // 128
            pre = begin_group(xT_d if col == 0 else ctxT_d, t0 if col == 0 else 0, n, [])
            pre_norm(0, col, n, 1, pre)
            gmlp(0, 0, col, n)
            post_norm(0, col, n, 1)
            mlp(0, col, n, hook=(lambda: prefetch_load(*nx[:4])) if nx else None)
            if debug:
                store_x(dbg_d[0], t0, n, [OUTD])
            pre_norm(1, col, n, 1)
            if nx and nx[4]:
                prefetch_stats(nx[2])
            for t in range(Tn):
                i = nxt("pqt", NPQ)
                for gr in range(4):
                    bk, BK = bank()
                    for mc in range(2):
                        S.op("pe", lambda e, bk=bk, gr=gr, mc=mc, t=t: e.matmul(
                            bk[:, :], lhsT=hT[:, gr * 2 + mc, t * 128:(t + 1) * 128], rhs=dftc[:, mc, :],
                            start=(mc == 0), stop=(mc == 1)), [H[gr * 2 + mc], TAB], [BK], inc=(mc == 1))
                    if gr % 2 == 0:
                        S.op("dve", lambda e, bk=bk, i=i, gr=gr: e.tensor_copy(out=pqt[i][:, gr * 512:(gr + 1) * 512],
                                                                              in_=bk[:, :]), [BK], [PQT[i]])
                    else:
                        S.op("act", lambda e, bk=bk, i=i, gr=gr: e.activation(out=pqt[i][:, gr * 512:(gr + 1) * 512],
                                                                             in_=bk[:, :], func=AF.Copy), [BK], [PQT[i]])
                ci = (t0 // 128 + t) if col == 0 else (32 + t)
                S.dma("pool", [lambda e, i=i, ci=ci: e.dma_start(out=pq_d[ci], in_=pqt[i][:, :])], [PQT[i]], [PQD[g]],
                      PQS[i])
            store_x(xs_d, t0, n, [XS[g]])

        for gi_, grp in enumerate(groups):
            if gi_ + 1 < len(groups):
                g2, t2, n2, c2 = groups[gi_ + 1]
                nx = (xT_d if c2 == 0 else ctxT_d, t2 if c2 == 0 else 0, n2, [], True)
            else:
                nx = (xs_d, 0, GT, [XS[0]], False)
            sweep1_group(*grp, nx)

        S.op("dve", lambda e: e.tensor_scalar(out=qkg[:, 0:2], in0=qkg[:, 0:2], scalar1=128.0 ** -0.5, scalar2=None,
                                              op0=ALU.mult), [SMALL], [SMALL])
        def sweep2_group(g, t0, n, col, nx):
            Tn = n // 128
            begin_group(xs_d, t0, n, [XS[g]])
            if col == 0:
                for ncn in range(32):
                    b = nxt("cst", NCS)
                    S.dma("sp", [lambda e, b=b, ncn=ncn: e.dma_start(out=cst[b], in_=dftl_d[g, ncn])], [],
                          [CST[b]] + VT, CST[b])
                    q_ = nxt("pqt", NPQ)
                    S.dma("sp", [lambda e, q_=q_, ncn=ncn: e.dma_start(out=pqt[q_][:, :], in_=pq_d[ncn])],
                          PQD[:NG], [PQT[q_]], PQT[q_])
                    for jc in range(8):
                        gr, half = jc // 2, jc % 2
                        S.op("pe", lambda e, jc=jc, gr=gr, half=half, q_=q_, b=b, ncn=ncn: e.matmul(
                            ps[jc][:, :], lhsT=pqt[q_][:, gr * 512 + half * 128: gr * 512 + half * 128 + 128],
                            rhs=cst[b][:, 0, :], start=(ncn == 0), stop=False), [PQT[q_], CST[b]], [PS[jc]], inc=False)
                        S.op("pe", lambda e, jc=jc, gr=gr, half=half, q_=q_, b=b, ncn=ncn: e.matmul(
                            ps[jc][:, :], lhsT=pqt[q_][:, gr * 512 + 256 + half * 128: gr * 512 + 256 + half * 128 + 128],
                            rhs=cst[b][:, 1, :], start=False, stop=(ncn == 31)), [PQT[q_], CST[b]], [PS[jc]],
                            inc=(jc == 7))
            else:
                for ncn in range(2):
                    q_ = nxt("pqt", NPQ)
                    S.dma("sp", [lambda e, q_=q_, ncn=ncn: e.dma_start(out=pqt[q_][:, :], in_=pq_d[32 + ncn])],
                          [PQD[NG]], [PQT[q_]], PQT[q_])
                    for jc in range(8):
                        gr, half = jc // 2, jc % 2
                        S.op("pe", lambda e, jc=jc, gr=gr, half=half, q_=q_, ncn=ncn: e.matmul(
                            ps[jc][:, :n], lhsT=pqt[q_][:, gr * 512 + half * 128: gr * 512 + half * 128 + 128],
                            rhs=dctx[:, ncn, 0, :], start=(ncn == 0), stop=False), [PQT[q_], TAB], [PS[jc]], inc=False)
                        S.op("pe", lambda e, jc=jc, gr=gr, half=half, q_=q_, ncn=ncn: e.matmul(
                            ps[jc][:, :n], lhsT=pqt[q_][:, gr * 512 + 256 + half * 128: gr * 512 + 256 + half * 128 + 128],
                            rhs=dctx[:, ncn, 1, :], start=False, stop=(ncn == 1)), [PQT[q_], TAB], [PS[jc]],
                            inc=(jc == 7))
            for jc in range(8):
                if jc % 2 == 0:
                    S.op("dve", lambda e, jc=jc: e.tensor_copy(out=hid[:, jc, :n], in_=ps[jc][:, :n]), [PS[jc]], [HID[jc]])
                else:
                    S.op("act", lambda e, jc=jc: e.activation(out=hid[:, jc, :n], in_=ps[jc][:, :n], func=AF.Copy),
                         [PS[jc]], [HID[jc]])
            rr["ps"] = 0
            proj_to_y(W_BOUT, bwout_b, 8, lambda kk: hid[:, kk, :n], HID, n)
            post_norm(1, col, n, 1)
            mlp(1, col, n, hook=(lambda: prefetch_load(*nx[:4])) if nx else None)
            if debug:
                store_x(dbg_d[1], t0, n, [OUTD])
            pre_norm(2, col, n, 1)
            if col == 0:
                S.dma("sp", [lambda e: e.dma_start(out=ropet[:, 0, :], in_=rope_d[0, :, t0:t0 + GT]),
                             lambda e: e.dma_start(out=ropet[:, 1, :], in_=rope_d[1, :, t0:t0 + GT])], [], [ROPE], ROPE)
            koff = (CTX + t0) if col == 0 else 0

            def head(pa, PA, ca, pb, PB, cb, kq, out_ap, OUT):
                bka, BKA = bank()
                for kc in range(8):
                    S.op("pe", lambda e, kc=kc: e.matmul(bka[:, :n], lhsT=pa[:, kc, ca:ca + 128], rhs=hT[:, kc, :n],
                                                         start=(kc == 0), stop=(kc == 7)), [PA, H[kc]], [BKA], inc=(kc == 7))
                if col == 0:
                    bkb, BKB = bank()
                    for kc in range(8):
                        S.op("pe", lambda e, kc=kc: e.matmul(bkb[:, :n], lhsT=pb[:, kc, cb:cb + 128], rhs=hT[:, kc, :n],
                                                             start=(kc == 0), stop=(kc == 7)), [PB, H[kc]], [BKB],
                             inc=(kc == 7))
                i = nxt("sq", NSQ)
                S.op("act", lambda e, i=i: e.activation(out=sq[i][:, :n], in_=bka[:, :n], func=AF.Square), [BKA], [SQ[i]])
                bkc, BKC = bank()
                S.op("pe", lambda e, i=i: e.matmul(bkc[:, :n], lhsT=ones[:, :], rhs=sq[i][:, :n], start=True, stop=True),
                     [SQ[i], CONST], [BKC], inc=True)
                r = nxt("rst", 2)
                S.op("act", lambda e, r=r: e.activation(out=rst[r][:, :n], in_=bkc[:, :n], func=AF.Sqrt,
                                                        bias=cvals[:, 0:1], scale=1.0 / 128), [BKC, CONST], [RST[r]])
                S.op("dve", lambda e, r=r: e.reciprocal(out=rst[r][:, :n], in_=rst[r][:, :n]), [RST[r]], [RST[r]])
                i1 = nxt("tmp", NTMP)
                if col == 0:
                    S.op("dve", lambda e, i1=i1: e.scalar_tensor_tensor(
                        out=tmp[i1][:, :n], in0=bka[:, :n], scalar=qkg[:, kq:kq + 1], in1=ropet[:, 0, :n],
                        op0=ALU.mult, op1=ALU.mult), [BKA, ROPE, SMALL], [TMP[i1]])
                    i2 = nxt("tmp", NTMP)
                    S.op("dve", lambda e, i2=i2: e.scalar_tensor_tensor(
                        out=tmp[i2][:, :n], in0=bkb[:, :n], scalar=qkg[:, kq + 1:kq + 2], in1=ropet[:, 1, :n],
                        op0=ALU.mult, op1=ALU.mult), [BKB, ROPE, SMALL], [TMP[i2]])
                    S.op("dve", lambda e, i1=i1, i2=i2: e.tensor_tensor(out=tmp[i1][:, :n], in0=tmp[i1][:, :n],
                                                                       in1=tmp[i2][:, :n], op=ALU.add),
                         [TMP[i1], TMP[i2]], [TMP[i1]])
                else:
                    S.op("dve", lambda e, i1=i1: e.tensor_scalar(out=tmp[i1][:, :n], in0=bka[:, :n], scalar1=qkg[:, kq:kq + 1],
                                                               scalar2=None, op0=ALU.mult), [BKA, SMALL], [TMP[i1]])
                S.op("dve", lambda e, i1=i1, r=r: e.tensor_tensor(out=out_ap, in0=tmp[i1][:, :n], in1=rst[r][:, :n],
                                                                op=ALU.mult), [TMP[i1], RST[r]], OUT)

            if col == 0:
                for half in range(2):
                    pa, PA = load_panel(W_QKV, qkv_b, 0, half * 512)
                    pb, PB = load_panel(W_QKV, qkv_b, 0, 1536 + half * 512)
                    for hh in range(4):
                        h = half * 4 + hh
                        head(pa, PA, hh * 128, pb, PB, hh * 128, 0, hid[:, h, :n], [HID[h]])
                S.dma("pool", [lambda e: e.dma_start(out=qT_dd[g], in_=hid[:, 0:8, :])], HID[0:8], [QTD[g]], STQ)
            pkv, PKV = load_panel(W_QKV, qkv_b, 0, 1024)
            if col == 0:
                pks, PKS = load_panel(W_QKV, qkv_b, 0, 2560, ncols=256)
            else:
                pks, PKS = pkv, PKV
            for kvh in range(2):
                head(pkv, PKV, kvh * 128, pks, PKS, kvh * 128, 2, KT[:, kvh, koff:koff + n], [KTt])
            for t in range(Tn):
                bk, BK = bank()
                for kc in range(8):
                    S.op("pe", lambda e, bk=bk, kc=kc, t=t: e.matmul(bk[:, 0:256], lhsT=hT[:, kc, t * 128:(t + 1) * 128],
                                                                     rhs=pkv[:, kc, 256:512], start=(kc == 0), stop=(kc == 7)),
                         [PKV, H[kc]], [BK], inc=(kc == 7))
                sc_i = koff // 128 + t
                S.op("act", lambda e, bk=bk, sc_i=sc_i: e.activation(out=Vs[:, sc_i, :], in_=bk[:, 0:256], func=AF.Copy),
                     [BK], [Vst])
            if col == 0:
                store_x(xs_d, t0, n, [XS[g]])

        for gi_, grp in enumerate(groups):
            if gi_ + 1 < len(groups):
                g2, t2, n2, c2 = groups[gi_ + 1]
                nx = (xs_d, t2, n2, [XS[g2]])
            else:
                nx = (xs_d, 0, GT, [XS[0]])
            sweep2_group(*grp, nx)

        load_ws(1)
        def load_q(gq):
            S.dma("sp", [lambda e: e.dma_start(out=hT[:, :, :], in_=qT_dd[gq])], [QTD[gq]], H, Hd)

        def sweep3_group(g, t0, n, col, nx):
            begin_group(xs_d, t0, n, [XS[g]])
            if not PRE.get("q"):
                load_q(g)
            PRE["q"] = False

            def attn_head(h):
                kvh = h // 4
                ob = 4 + 2 * (h % 2)
                bo, BO = ps[ob], PS[ob]
                bd, BD = ps[ob + 1], PS[ob + 1]
                sbk = [None] * NSC

                def s_mm(sc):
                    bi = sc % 4
                    bk, BK = ps[bi], PS[bi]
                    sbk[sc] = (bk, BK)
                    S.op("pe", lambda e, bk=bk, sc=sc: e.matmul(bk[:, :], lhsT=KT[:, kvh, sc * 128:(sc + 1) * 128],
                                                                rhs=hT[:, h, :], start=True, stop=True),
                         [KTt, H[h]], [BK], inc=True)
                s_mm(0)
                s_mm(1)
                for sc in range(NSC):
                    bk, BK = sbk[sc]
                    i = nxt("pt", NPT)
                    S.op("act", lambda e, bk=bk, i=i: e.activation(out=pt[i][:, :], in_=bk[:, :], func=AF.Exp,
                                                                   bias=cvals[:, 2:3], scale=1.0), [BK, CONST], [PT[i]])
                    if sc + 2 < NSC:
                        s_mm(sc + 2)
                    S.op("pe", lambda e, i=i, sc=sc: e.matmul(bo[:, :], lhsT=Vs[:, sc, kvh * 128:(kvh + 1) * 128],
                                                              rhs=pt[i][:, :], start=(sc == 0), stop=(sc == NSC - 1)),
                         [Vst, PT[i]], [BO], inc=False)
                    S.op("pe", lambda e, i=i, sc=sc: e.matmul(bd[:, :], lhsT=ones[:, :], rhs=pt[i][:, :],
                                                              start=(sc == 0), stop=(sc == NSC - 1)),
                         [CONST, PT[i]], [BD], inc=True)
                i = nxt("tmp", NTMP)
                S.op("dve", lambda e, i=i: e.reciprocal(out=tmp[i][:, :], in_=bd[:, :]), [BD], [TMP[i]])
                S.op("dve", lambda e, i=i, h=h: e.tensor_tensor(out=hid[:, h, :], in0=bo[:, :], in1=tmp[i][:, :],
                                                              op=ALU.mult), [BO, TMP[i]], [HID[h]])
            for h in range(8):
                attn_head(h)
            proj_to_y(W_WO, wo_b, 8, lambda kk: hid[:, kk, :n], HID, n)
            post_norm(2, 0, n, 1)
            mlp(2, 0, n)
            if debug:
                store_x(dbg_d[2], t0, n, [OUTD])
            pre_norm(3, 0, n, 1)
            gmlp(3, 1, 0, n)
            post_norm(3, 0, n, 1)

            def hook3():
                prefetch_load(*nx[:4])
                load_q(nx[4])
                PRE["q"] = True
            mlp(3, 0, n, hook=hook3 if nx else None)
            store_x(out_d, t0, n, [OUTD])

        for gi_, grp in enumerate(groups[:NG]):
            if gi_ + 1 < NG:
                g2, t2, n2, c2 = groups[gi_ + 1]
                nx = (xs_d, t2, n2, [XS[g2]], g2)
            else:
                nx = None
            sweep3_group(*grp, nx)

        S.final_wait("sp")
        S.emit()
    return nc


def _const_tables():
    bf = ml_dtypes.bfloat16
    m = np.arange(256, dtype=np.float64)
    ang = 2 * np.pi * np.outer(m, m) / 256.0
    dc = np.concatenate([np.cos(ang), np.sin(ang)], axis=1) / 16.0
    dft_c = dc.reshape(2, 128, 512).transpose(1, 0, 2).astype(np.float32).astype(bf)
    nk = (np.outer(np.arange(SEQ, dtype=np.int64), np.arange(SEQ, dtype=np.int64)) % SEQ).astype(np.int32)
    cvec = (np.cos(2 * np.pi * np.arange(SEQ) / SEQ) / 64.0).astype(np.float32).astype(bf)
    svec = (-np.sin(2 * np.pi * np.arange(SEQ) / SEQ) / 64.0).astype(np.float32).astype(bf)
    ct = cvec[nk].reshape(32, 128, 8, 512)
    stt = svec[nk].reshape(32, 128, 8, 512)
    dft_l = np.ascontiguousarray(np.stack([ct, stt], axis=3).transpose(2, 0, 1, 3, 4))
    n = np.arange(256, dtype=np.float64)
    ac = 2 * np.pi * np.outer(n, n) / 256.0
    dctx = np.stack([np.cos(ac), -np.sin(ac)], axis=1) / 16.0
    dctx = dctx.reshape(2, 128, 2, 256).transpose(1, 0, 2, 3)
    t = np.arange(SEQ)
    row = (t // 64).astype(np.float32)
    colp = (t % 64).astype(np.float32)
    inv = (np.float32(10000.0) ** (-np.arange(32, dtype=np.float32) / np.float32(32))).astype(np.float32)
    angr = np.concatenate([row[:, None] * inv, colp[:, None] * inv], axis=-1).astype(np.float32)
    cs, sn = np.cos(angr), np.sin(angr)
    cosT = np.concatenate([cs, cs], axis=1).T
    sinT = np.concatenate([-sn, sn], axis=1).T
    rope = np.stack([cosT, sinT], axis=0)
    return dict(dft_c=dft_c, dft_l=dft_l,
                dft_ctx=dctx.astype(np.float32).astype(bf), rope=np.ascontiguousarray(rope.astype(np.float32)))


def _prep_shared(c_ctx, ada_w, ada_b, norm_g, mlp_w1, mlp_w2, a_w_in, a_ln_g, a_w_s, a_b_s, a_w_out, b_w_out,
                 c_w_qkv, c_q_g, c_k_g, c_w_o):
    f = np.float32
    sw = (np.arange(128) + 64) % 128
    wq = c_w_qkv[0][:, :1024].reshape(D, 8, 128)
    wk = c_w_qkv[0][:, 1024:1280].reshape(D, 2, 128)
    qkv_ext = np.concatenate([c_w_qkv[0], wq[:, :, sw].reshape(D, 1024), wk[:, :, sw].reshape(D, 256)], axis=1)
    qkg = np.stack([c_q_g[0], c_q_g[0][sw], c_k_g[0], c_k_g[0][sw]], axis=1)
    d = dict(
        ada_w=np.ascontiguousarray(ada_w, dtype=f),
        ada_b=np.ascontiguousarray(ada_b.reshape(4, 48, 128).transpose(2, 0, 1), dtype=f),
        norm_g=np.ascontiguousarray(norm_g.reshape(4, 4, 8, 128).transpose(3, 0, 1, 2), dtype=f),
        mlp_w1=np.ascontiguousarray(mlp_w1, dtype=f), mlp_w2=np.ascontiguousarray(mlp_w2, dtype=f),
        a_w_in=np.ascontiguousarray(a_w_in, dtype=f),
        a_ln_g=np.ascontiguousarray(a_ln_g.reshape(2, 16, 128).transpose(2, 0, 1), dtype=f),
        a_w_s=np.ascontiguousarray(a_w_s.transpose(0, 3, 1, 2), dtype=f),
        a_b_s=np.ascontiguousarray(np.broadcast_to(a_b_s[:, None, :, :], (2, 128, 8, 128)), dtype=f),
        a_w_out=np.ascontiguousarray(a_w_out, dtype=f), b_w_out=np.ascontiguousarray(b_w_out[0], dtype=f),
        c_w_qkv=np.ascontiguousarray(qkv_ext, dtype=f), c_qk_g=np.ascontiguousarray(qkg, dtype=f),
        c_w_o=np.ascontiguousarray(c_w_o[0], dtype=f),
    )
    d.update(_const_tables())
    return d


def _core_inputs(shared, x_b, c_b, ctx_b, c_ctx):
    m = dict(shared)
    m["xT"] = np.ascontiguousarray(x_b.T, dtype=np.float32)
    m["ctxT"] = np.ascontiguousarray(ctx_b.T, dtype=np.float32)
    m["cc"] = np.ascontiguousarray(np.stack([c_b.reshape(8, 128).T, c_ctx.reshape(8, 128).T], axis=2), dtype=np.float32)
    return m


def kernel(x, c, ctx, c_ctx, ada_w, ada_b, norm_g, mlp_w1, mlp_w2, a_w_in, a_ln_g, a_w_s, a_b_s, a_w_out,
           b_w_out, c_w_qkv, c_q_g, c_k_g, c_w_o):
    args = [np.asarray(a) for a in (c_ctx, ada_w, ada_b, norm_g, mlp_w1, mlp_w2, a_w_in, a_ln_g, a_w_s, a_b_s, a_w_out,
                                    b_w_out, c_w_qkv, c_q_g, c_k_g, c_w_o)]
    x = np.asarray(x); c = np.asarray(c); ctx = np.asarray(ctx)
    shared = _prep_shared(*args)
    nc = build_program(debug=False)
    nb = x.shape[0]
    in_maps = [_core_inputs(shared, x[b], c[b], ctx[b], args[0]) for b in range(nb)]
    res = run_bass_kernel_spmd(nc, in_maps, core_ids=list(range(nb)))
    out = np.stack([np.asarray(r["outT"]).T for r in res.results], axis=0)
    return np.ascontiguousarray(out, dtype=np.float32)
```

```python
from contextlib import ExitStack
import numpy as np
import ml_dtypes
import concourse.bass as bass
import concourse.mybir as mybir
from concourse.bass_utils import run_bass_kernel_spmd

F32 = mybir.dt.float32
BF16 = mybir.dt.bfloat16
AF = mybir.ActivationFunctionType
ALU = mybir.AluOpType

D = 1024
SEQ = 4096
CTX = 256
NG = 8
GT = 512
NKEY = CTX + SEQ
NSC = NKEY // 128
QKVW = 2816


class T:
    __slots__ = ("name", "w", "r", "dsem", "dcnt")

    def __init__(self, name, dsem=None):
        self.name = name
        self.w = {}
        self.r = {}
        self.dsem = dsem
        self.dcnt = 0


class Sched:
    QS = ("pe", "act", "dve", "pool", "sp")

    def __init__(self, nc, stack):
        self.nc = nc
        self.stack = stack
        self.prog = {q: [] for q in self.QS}
        self.semh = {}
        self.cnt = {}
        for q in ("pe", "act", "dve", "pool"):
            self.semh[q] = stack.enter_context(nc.semaphore("s_" + q))
            self.cnt[q] = 0
        self.seen = {q: {} for q in self.QS}
        self.dma_objs = []

    def tracked(self, name, dma=False):
        t = T(name)
        if dma:
            key = "d_" + name
            self.semh[key] = self.stack.enter_context(self.nc.semaphore(key))
            t.dsem = key
            self.dma_objs.append(t)
        return t

    def _waits(self, q, reads, writes):
        need = {}
        for t in reads:
            for s, v in t.w.items():
                if need.get(s, 0) < v:
                    need[s] = v
        for t in writes:
            for s, v in t.w.items():
                if need.get(s, 0) < v:
                    need[s] = v
            for s, v in t.r.items():
                if need.get(s, 0) < v:
                    need[s] = v
        out = []
        seen = self.seen[q]
        for s, v in need.items():
            if q == "pe" and s == "pe":
                continue
            if seen.get(s, 0) < v:
                seen[s] = v
                out.append((s, v))
        return out

    def _reg(self, ev, reads, writes):
        s, v = ev
        for t in reads:
            if t.r.get(s, 0) < v:
                t.r[s] = v
        for t in writes:
            t.w = {s: v}
            t.r = {}

    def op(self, q, fn, reads=(), writes=(), inc=True):
        waits = self._waits(q, reads, writes)
        v = self.cnt[q] + 1
        if inc:
            self.cnt[q] = v
        self.prog[q].append((waits, fn, (q, 1) if inc else None))
        self._reg((q, v), reads, writes)

    def dma(self, q, fns, reads, writes, semobj):
        waits = self._waits(q, reads, writes)
        semobj.dcnt += len(fns)
        ev = (semobj.dsem, 16 * semobj.dcnt)
        for i, fn in enumerate(fns):
            self.prog[q].append((waits if i == 0 else [], fn, (semobj.dsem, 16)))
        self._reg(ev, reads, writes)

    def final_wait(self, q="sp"):
        waits = []
        for t in self.dma_objs:
            if t.dcnt:
                waits.append((t.dsem, 16 * t.dcnt))
        for e in ("pe", "act", "dve", "pool"):
            if self.cnt[e]:
                waits.append((e, self.cnt[e]))
        self.prog[q].append((waits, None, None))

    def emit(self):
        nc = self.nc
        semh = self.semh
        prog = self.prog

        def replay(q, eng):
            for waits, fn, inc in prog[q]:
                for s, v in waits:
                    eng.wait_ge(semh[s], v)
                if fn is None:
                    continue
                ins = fn(eng)
                if inc is not None:
                    ins.then_inc(semh[inc[0]], inc[1])

        with nc.Block() as block:
            @block.tensor
            def _(e):
                replay("pe", e)

            @block.scalar
            def _(e):
                replay("act", e)

            @block.vector
            def _(e):
                replay("dve", e)

            @block.gpsimd
            def _(e):
                replay("pool", e)

            @block.sync
            def _(e):
                replay("sp", e)


def build_program(debug=False):
    nc = bass.Bass("TRN2", target_bir_lowering=False)

    def din(name, shape, dt=F32):
        return nc.dram_tensor(name, list(shape), dt, kind="ExternalInput").ap()

    def dint(name, shape, dt):
        return nc.dram_tensor(name, list(shape), dt, kind="Internal").ap()

    xT_d = din("xT", [D, SEQ])
    ctxT_d = din("ctxT", [D, CTX])
    cc_d = din("cc", [128, 8, 2])
    adaw_d = din("ada_w", [4, D, 6 * D])
    adab_d = din("ada_b", [128, 4, 48])
    ng_d = din("norm_g", [128, 4, 4, 8])
    w1_d = din("mlp_w1", [4, D, 4 * D])
    w2_d = din("mlp_w2", [4, 4 * D, D])
    awin_d = din("a_w_in", [2, D, 4 * D])
    alng_d = din("a_ln_g", [128, 2, 16])
    aws_d = din("a_w_s", [2, 128, 8, 128])
    abs_d = din("a_b_s", [2, 128, 8, 128])
    awout_d = din("a_w_out", [2, 2 * D, D])
    bwout_d = din("b_w_out", [D, D])
    qkv_d = din("c_w_qkv", [D, QKVW])
    qkg_d = din("c_qk_g", [128, 4])
    wo_d = din("c_w_o", [D, D])
    dftc_d = din("dft_c", [128, 2, 512], BF16)
    dftl_d = din("dft_l", [8, 32, 128, 2, 512], BF16)
    dctx_d = din("dft_ctx", [128, 2, 2, 256], BF16)
    rope_d = din("rope", [2, 128, SEQ])
    out_d = nc.dram_tensor("outT", [D, SEQ], F32, kind="ExternalOutput").ap()
    if debug:
        dbg_d = nc.dram_tensor("dbg", [3, D, SEQ + CTX], F32, kind="ExternalOutput").ap()

    w1_b = dint("w1_b", [4, D, 4 * D], BF16)
    w2_b = dint("w2_b", [4, 4 * D, D], BF16)
    awin_b = dint("awin_b", [2, D, 4 * D], BF16)
    awout_b = dint("awout_b", [2, 2 * D, D], BF16)
    bwout_b = dint("bwout_b", [D, D], BF16)
    qkv_b = dint("qkv_b", [D, QKVW], BF16)
    wo_b = dint("wo_b", [D, D], BF16)
    xs_d = dint("xs", [D, SEQ + CTX], F32)
    pq_d = dint("pq", [34, 128, 2048], BF16)
    qT_dd = dint("qTs", [NG, 128, 8, GT], BF16)

    with ExitStack() as st:
        S = Sched(nc, st)

        def sb(name, shape, dt):
            return st.enter_context(nc.sbuf_tensor("sb_" + name, list(shape), dt))

        xTs = [sb(f"xT{i}", [128, 8, GT], F32) for i in range(2)]
        Xs = [[S.tracked(f"x{i}_{c}") for c in range(8)] for i in range(2)]
        Xds = [S.tracked(f"xdma{i}", dma=True) for i in range(2)]
        CX = {"i": 0}
        hT = sb("hT", [128, 8, GT], BF16)
        H = [S.tracked(f"h{c}") for c in range(8)]
        Hd = S.tracked("hdma", dma=True)
        yT = sb("yT", [128, 8, GT], F32)
        Y = [S.tracked(f"y{c}") for c in range(8)]
        hid = sb("hid", [128, 32, GT], BF16)
        HID = [S.tracked(f"hid{c}") for c in range(32)]
        NSQ = 3
        sq = [sb(f"sq{i}", [128, GT], BF16) for i in range(NSQ)]
        SQ = [S.tracked(f"sq{i}") for i in range(NSQ)]
        NTMP = 4
        tmp = [sb(f"tmp{i}", [128, GT], F32) for i in range(NTMP)]
        TMP = [S.tracked(f"tmp{i}") for i in range(NTMP)]
        rst = [sb(f"rst{i}", [128, GT], F32) for i in range(2)]
        RST = [S.tracked(f"rst{i}") for i in range(2)]
        NW = 3
        pan = [sb(f"pan{i}", [128, 8, 512], BF16) for i in range(NW)]
        PAN = [S.tracked(f"pan{i}", dma=True) for i in range(NW)]
        PANB = [S.tracked(f"panb{i}") for i in range(NW)]
        ov = sb("ov", [128, 8192], BF16)
        vt = ov[:, :].rearrange("p (t f) -> p t f", t=4)
        VT = [S.tracked(f"vt{t}") for t in range(4)]
        st1 = sb("st1", [128, 4, 4], F32)
        st2 = sb("st2", [128, 4, 4], F32)
        stm = sb("stm", [128, 8, 4], F32)
        ST = S.tracked("st")
        KT = sb("KT", [128, 2, NKEY], BF16)
        KTt = S.tracked("KT")
        Vs = sb("Vs", [128, NSC, 256], BF16)
        Vst = S.tracked("Vs")
        NPT = 3
        pt = [sb(f"pt{i}", [128, GT], BF16) for i in range(NPT)]
        PT = [S.tracked(f"pt{i}") for i in range(NPT)]
        ropet = sb("ropet", [128, 2, GT], F32)
        ROPE = S.tracked("rope", dma=True)
        NPQ = 2
        pqt = [sb(f"pqt{i}", [128, 2048], BF16) for i in range(NPQ)]
        PQT = [S.tracked(f"pqt{i}", dma=True) for i in range(NPQ)]
        NCS = 6
        cst = [ov[:, i * 1024:(i + 1) * 1024].rearrange("p (a k) -> p a k", a=2) for i in range(NCS)]
        CST = [S.tracked(f"cst{i}", dma=True) for i in range(NCS)]
        ones = sb("ones", [128, 128], BF16)
        cvals = sb("cvals", [128, 4], F32)
        CONST = S.tracked("const")
        cc = sb("cc", [128, 8, 2], F32)
        ccb = sb("ccb", [128, 8, 2], BF16)
        CC = S.tracked("cc", dma=True)
        CCB = S.tracked("ccb")
        adab = sb("adab", [128, 4, 48], F32)
        ngt = sb("ngt", [128, 4, 4, 8], F32)
        alng = sb("alng", [128, 2, 16], F32)
        qkg = sb("qkg", [128, 4], F32)
        SMALL = S.tracked("small", dma=True)
        dftc = sb("dftc", [128, 2, 512], BF16)
        dctx = sb("dctx", [128, 2, 2, 256], BF16)
        TAB = S.tracked("tab", dma=True)
        wsb = sb("wsb", [128, 8, 128], BF16)
        bsb = sb("bsb", [128, 8, 128], F32)
        WSF = S.tracked("wsf", dma=True)
        WSB = S.tracked("wsb")
        mods = sb("mods", [128, 4, 2, 48], F32)
        MODS = [S.tracked(f"mods{l}") for l in range(4)]
        dsc = sb("dsc", [128, 4, 2, 4, 8], F32)
        DSC = [S.tracked(f"dsc{l}") for l in range(4)]

        ps = [st.enter_context(nc.psum_tensor(f"ps{i}", [128, 512], F32)) for i in range(8)]
        PS = [S.tracked(f"ps{i}") for i in range(8)]
        rr = {"ps": 0, "pan": 0, "sq": 0, "tmp": 0, "rst": 0, "pt": 0, "pqt": 0, "cst": 0}

        def nxt(key, n):
            i = rr[key]
            rr[key] = (i + 1) % n
            return i

        modprog = [0, 0, 0, 0]

        def bank():
            nb_ = 7 if min(modprog) < 6 else 8
            i = rr["ps"] % nb_
            rr["ps"] = (i + 1) % nb_
            return ps[i], PS[i]

        W_W1 = [S.tracked(f"Ww1{l}", dma=True) for l in range(4)]
        W_W2 = [S.tracked(f"Ww2{l}", dma=True) for l in range(4)]
        W_AIN = [S.tracked(f"Wain{j}", dma=True) for j in range(2)]
        W_AOUT = [S.tracked(f"Waout{j}", dma=True) for j in range(2)]
        W_BOUT = S.tracked("Wbout", dma=True)
        W_QKV = S.tracked("Wqkv", dma=True)
        W_WO = S.tracked("Wwo", dma=True)
        XS = [S.tracked(f"XS{g}") for g in range(NG + 1)]
        PQD = [S.tracked(f"PQD{g}") for g in range(NG + 1)]
        QTD = [S.tracked(f"QTD{g}") for g in range(NG)]
        OUTD = S.tracked("outd")
        STOs = [S.tracked(f"sto{i}", dma=True) for i in range(2)]
        PQS = [S.tracked(f"pqs{i}", dma=True) for i in range(NPQ)]
        STQ = S.tracked("stq", dma=True)

        def conv(dst, src, rows, cols, obj, maxel=1 << 20):
            step = max(1, maxel // cols)
            fns = []
            for r0 in range(0, rows, step):
                r1 = min(rows, r0 + step)
                fns.append(lambda e, r0=r0, r1=r1: e.dma_start(out=dst[r0:r1, :], in_=src[r0:r1, :]))
            S.dma("pool", fns, [], [obj], obj)

        S.op("pool", lambda e: e.memset(ones[:], 1.0), [], [CONST])
        S.op("pool", lambda e: e.memset(cvals[:, 0:1], 1e-6), [], [CONST])
        S.op("pool", lambda e: e.memset(cvals[:, 1:2], 1e-5), [], [CONST])
        S.op("pool", lambda e: e.memset(cvals[:, 2:3], -11.313708498984761), [], [CONST])
        S.dma("sp", [lambda e: e.dma_start(out=cc[:], in_=cc_d[:, :, :])], [], [CC], CC)
        S.dma("sp", [lambda e: e.dma_start(out=adab[:], in_=adab_d[:, :, :]),
                     lambda e: e.dma_start(out=ngt[:], in_=ng_d[:, :, :, :]),
                     lambda e: e.dma_start(out=alng[:], in_=alng_d[:, :, :]),
                     lambda e: e.dma_start(out=qkg[:], in_=qkg_d[:, :])], [], [SMALL], SMALL)
        S.dma("sp", [lambda e: e.dma_start(out=dftc[:], in_=dftc_d[:, :, :]),
                     lambda e: e.dma_start(out=dctx[:], in_=dctx_d[:, :, :, :])], [], [TAB], TAB)

        conv(awin_b[0], awin_d[0], D, 4 * D, W_AIN[0])
        conv(awout_b[0], awout_d[0], 2 * D, D, W_AOUT[0])
        conv(w1_b[0], w1_d[0], D, 4 * D, W_W1[0])
        conv(w2_b[0], w2_d[0], 4 * D, D, W_W2[0])
        conv(bwout_b, bwout_d, D, D, W_BOUT)
        conv(w1_b[1], w1_d[1], D, 4 * D, W_W1[1])
        conv(w2_b[1], w2_d[1], 4 * D, D, W_W2[1])
        conv(qkv_b, qkv_d, D, QKVW, W_QKV)
        conv(wo_b, wo_d, D, D, W_WO)
        conv(w1_b[2], w1_d[2], D, 4 * D, W_W1[2])
        conv(w2_b[2], w2_d[2], 4 * D, D, W_W2[2])
        conv(awin_b[1], awin_d[1], D, 4 * D, W_AIN[1])
        conv(awout_b[1], awout_d[1], 2 * D, D, W_AOUT[1])
        conv(w1_b[3], w1_d[3], D, 4 * D, W_W1[3])
        conv(w2_b[3], w2_d[3], 4 * D, D, W_W2[3])

        def load_panel(wobj, wap, r0, c0, kc=8, ncols=512, filler=True):
            if filler:
                filler_step()
            i = nxt("pan", NW)
            src = wap[r0:r0 + kc * 128, c0:c0 + ncols].rearrange("(kc p) n -> p kc n", p=128)
            S.dma("sp", [lambda e: e.dma_start(out=pan[i][:, 0:kc, 0:ncols], in_=src)], [wobj], [PAN[i], PANB[i]], PAN[i])
            return pan[i], PAN[i]

        S.op("act", lambda e: e.activation(out=ccb[:], in_=cc[:], func=AF.Silu), [CC], [CCB])
        def mods_gen(l):
            derived = {1: (0, 0, False), 2: (1, 1, True), 4: (2, 2, False), 5: (3, 3, True)}
            for pn in range(48):
                i = nxt("pan", NW)
                flat = pan[i][:, :, :].rearrange("p a b -> p (a b)")
                pf = flat[:, 0:2048].bitcast(F32).rearrange("p (kc n) -> p kc n", kc=8)
                pb = flat[:, 2048:3072].rearrange("p (kc n) -> p kc n", kc=8)
                src = adaw_d[l][:, pn * 128:(pn + 1) * 128].rearrange("(kc p) n -> p kc n", p=128)
                S.dma("sp", [lambda e, pf=pf, src=src: e.dma_start(out=pf, in_=src)], [], [PAN[i], PANB[i]], PAN[i])
                S.op("dve", lambda e, pf=pf, pb=pb: e.tensor_copy(out=pb, in_=pf), [PAN[i]], [PANB[i]])
                bk, BK = ps[7], PS[7]
                for kc in range(8):
                    S.op("pe", lambda e, bk=bk, pb=pb, kc=kc: e.matmul(
                        bk[:, 0:2], lhsT=pb[:, kc, :], rhs=ccb[:, kc, :], start=(kc == 0), stop=(kc == 7)),
                        [PANB[i], CCB], [BK], inc=(kc == 7))
                S.op("dve", lambda e, bk=bk, l=l, pn=pn: e.tensor_scalar(
                    out=mods[:, l, :, pn], in0=bk[:, 0:2], scalar1=adab[:, l, pn:pn + 1], scalar2=None, op0=ALU.add),
                    [BK, SMALL], [MODS[l]])
                if pn % 8 == 7:
                    m = pn // 8
                    if m in derived:
                        k, gi, is_gate = derived[m]
                        for col in range(2):
                            src2 = mods[:, l, col, m * 8:(m + 1) * 8]
                            if is_gate:
                                S.op("dve", lambda e, src2=src2, l=l, col=col, k=k, gi=gi: e.tensor_tensor(
                                    out=dsc[:, l, col, k, :], in0=src2, in1=ngt[:, l, gi, :], op=ALU.mult),
                                    [MODS[l], SMALL], [DSC[l]])
                            else:
                                S.op("dve", lambda e, src2=src2, l=l, col=col, k=k, gi=gi: e.scalar_tensor_tensor(
                                    out=dsc[:, l, col, k, :], in0=src2, scalar=1.0, in1=ngt[:, l, gi, :],
                                    op0=ALU.add, op1=ALU.mult), [MODS[l], SMALL], [DSC[l]])
                    modprog[l] = m + 1
                yield

        gens = [mods_gen(l) for l in range(4)]
        fill = {"k": 0}

        def advance(l):
            try:
                next(gens[l])
            except StopIteration:
                pass

        def filler_step():
            fill["k"] += 1
            for l in range(4):
                if modprog[l] < 6:
                    for _ in range(3 if l <= 1 else 1):
                        advance(l)
                    return

        def need_mods(l, m):
            while modprog[l] <= m:
                advance(l)

        def sc_A1(l, col, c): return dsc[:, l, col, 0, c:c + 1]
        def sc_G1(l, col, c): return dsc[:, l, col, 1, c:c + 1]
        def sc_A2(l, col, c): return dsc[:, l, col, 2, c:c + 1]
        def sc_G2(l, col, c): return dsc[:, l, col, 3, c:c + 1]
        def sc_B1(l, col, c): return mods[:, l, col, 0 * 8 + c:0 * 8 + c + 1]
        def sc_B2(l, col, c): return mods[:, l, col, 3 * 8 + c:3 * 8 + c + 1]

        def rms_rstd(srcs, SRCS, n, nfeat, eps_col=0):
            bk, BK = bank()
            nchunk = len(srcs)
            for c in range(nchunk):
                i = nxt("sq", NSQ)
                S.op("act", lambda e, i=i, c=c: e.activation(out=sq[i][:, :n], in_=srcs[c], func=AF.Square),
                     [SRCS[c]], [SQ[i]])
                S.op("pe", lambda e, i=i, c=c, bk=bk: e.matmul(bk[:, :n], lhsT=ones[:, :], rhs=sq[i][:, :n],
                                                             start=(c == 0), stop=(c == nchunk - 1)),
                     [SQ[i], CONST], [BK], inc=True)
            r = nxt("rst", 2)
            S.op("act", lambda e, r=r, bk=bk: e.activation(out=rst[r][:, :n], in_=bk[:, :n], func=AF.Sqrt,
                                                          bias=cvals[:, eps_col:eps_col + 1], scale=1.0 / nfeat),
                 [BK, CONST], [RST[r]])
            S.op("dve", lambda e, r=r: e.reciprocal(out=rst[r][:, :n], in_=rst[r][:, :n]), [RST[r]], [RST[r]])
            return rst[r], RST[r]

        def x_stats(n, xb):
            xT, X = xTs[xb], Xs[xb]
            return rms_rstd([xT[:, c, :n] for c in range(8)], X, n, D)

        def pre_norm(l, col, n, which, pre=None):
            need_mods(l, 1 if which == 1 else 4)
            xT, X = xTs[CX["i"]], Xs[CX["i"]]
            R, RT = pre if pre is not None else x_stats(n, CX["i"])
            for c in range(8):
                i = nxt("tmp", NTMP)
                S.op("dve", lambda e, i=i, c=c: e.tensor_tensor(out=tmp[i][:, :n], in0=xT[:, c, :n], in1=R[:, :n],
                                                              op=ALU.mult), [X[c], RT], [TMP[i]])
                a = sc_A1(l, col, c) if which == 1 else sc_A2(l, col, c)
                b = sc_B1(l, col, c) if which == 1 else sc_B2(l, col, c)
                S.op("act", lambda e, i=i, c=c, a=a, b=b: e.activation(out=hT[:, c, :n], in_=tmp[i][:, :n],
                                                                      func=AF.Identity, bias=b, scale=a),
                     [TMP[i], DSC[l], MODS[l]], [H[c]])

        def post_norm(l, col, n, which):
            need_mods(l, 2 if which == 1 else 5)
            xT, X = xTs[CX["i"]], Xs[CX["i"]]
            R, RT = rms_rstd([yT[:, c, :n] for c in range(8)], Y, n, D)
            for c in range(8):
                i = nxt("tmp", NTMP)
                S.op("dve", lambda e, i=i, c=c: e.tensor_tensor(out=tmp[i][:, :n], in0=yT[:, c, :n], in1=R[:, :n],
                                                              op=ALU.mult), [Y[c], RT], [TMP[i]])
                g = sc_G1(l, col, c) if which == 1 else sc_G2(l, col, c)
                S.op("dve", lambda e, i=i, c=c, g=g: e.scalar_tensor_tensor(
                    out=xT[:, c, :n], in0=tmp[i][:, :n], scalar=g, in1=xT[:, c, :n], op0=ALU.mult, op1=ALU.add),
                    [TMP[i], DSC[l], X[c]], [X[c]])

        def proj_to_y(wobj, wap, kchunks, rhs_fn, RHS, n):
            nkb = kchunks // 8
            for cp in range(2):
                bks = [bank() for _ in range(4)]
                for kb in range(nkb):
                    p_, P_ = load_panel(wobj, wap, kb * 1024, cp * 512)
                    for jj in range(4):
                        bk, BK = bks[jj]
                        for kc in range(8):
                            kk = kb * 8 + kc
                            last = (kb == nkb - 1 and kc == 7)
                            S.op("pe", lambda e, bk=bk, p_=p_, jj=jj, kc=kc, kk=kk, last=last: e.matmul(
                                bk[:, :n], lhsT=p_[:, kc, jj * 128:(jj + 1) * 128], rhs=rhs_fn(kk),
                                start=(kk == 0), stop=last), [P_, RHS[kk]], [BK], inc=(kc == 7))
                for jj in range(4):
                    j = cp * 4 + jj
                    bk, BK = bks[jj]
                    if jj % 2 == 0:
                        S.op("dve", lambda e, bk=bk, j=j: e.tensor_copy(out=yT[:, j, :n], in_=bk[:, :n]), [BK], [Y[j]])
                    else:
                        S.op("act", lambda e, bk=bk, j=j: e.activation(out=yT[:, j, :n], in_=bk[:, :n], func=AF.Copy),
                             [BK], [Y[j]])

        def mlp(l, col, n, hook=None):
            pre_norm(l, col, n, 2)
            for pn in range(8):
                p_, P_ = load_panel(W_W1[l], w1_b[l], 0, pn * 512)
                for jj in range(4):
                    hc = pn * 4 + jj
                    bk, BK = bank()
                    for kc in range(8):
                        S.op("pe", lambda e, bk=bk, p_=p_, jj=jj, kc=kc: e.matmul(
                            bk[:, :n], lhsT=p_[:, kc, jj * 128:(jj + 1) * 128], rhs=hT[:, kc, :n],
                            start=(kc == 0), stop=(kc == 7)), [P_, H[kc]], [BK], inc=(kc == 7))
                    i = nxt("tmp", NTMP)
                    S.op("act", lambda e, bk=bk, i=i: e.activation(out=tmp[i][:, :n], in_=bk[:, :n], func=AF.Relu),
                         [BK], [TMP[i]])
                    S.op("dve", lambda e, i=i, hc=hc: e.tensor_tensor(out=hid[:, hc, :n], in0=tmp[i][:, :n],
                                                                   in1=tmp[i][:, :n], op=ALU.mult),
                         [TMP[i]], [HID[hc]])
            proj_to_y(W_W2[l], w2_b[l], 32, lambda kk: hid[:, kk, :n], HID, n)
            if hook is not None:
                hook()
            post_norm(l, col, n, 2)

        def gmlp(l, j, col, n):
            Tn = n // 128
            for pn in range(4):
                p_, P_ = load_panel(W_AIN[j], awin_b[j], 0, pn * 512)
                for jj in range(4):
                    fc = pn * 4 + jj
                    bk, BK = bank()
                    for kc in range(8):
                        S.op("pe", lambda e, bk=bk, p_=p_, jj=jj, kc=kc: e.matmul(
                            bk[:, :n], lhsT=p_[:, kc, jj * 128:(jj + 1) * 128], rhs=hT[:, kc, :n],
                            start=(kc == 0), stop=(kc == 7)), [P_, H[kc]], [BK], inc=(kc == 7))
                    S.op("act", lambda e, bk=bk, fc=fc: e.activation(out=hid[:, fc, :n], in_=bk[:, :n],
                                                                    func=AF.Gelu_apprx_tanh), [BK], [HID[fc]])
            S.op("dve", lambda e: e.memset(st1[:], 0.0), [], [ST])
            S.op("dve", lambda e: e.memset(st2[:], 0.0), [], [ST])
            for pv in range(4):
                p_, P_ = load_panel(W_AIN[j], awin_b[j], 0, 2048 + pv * 512)
                for t in range(Tn):
                    bk, BK = bank()
                    for kc in range(8):
                        S.op("pe", lambda e, bk=bk, p_=p_, t=t, kc=kc: e.matmul(
                            bk[:, :], lhsT=hT[:, kc, t * 128:(t + 1) * 128], rhs=p_[:, kc, :],
                            start=(kc == 0), stop=(kc == 7)), [P_, H[kc]], [BK], inc=(kc == 7))
                    S.op("act", lambda e, bk=bk, t=t, pv=pv: e.activation(
                        out=vt[:, t, pv * 512:(pv + 1) * 512], in_=bk[:, :], func=AF.Gelu_apprx_tanh,
                        accum_out=st1[:, t, pv:pv + 1]), [BK, ST], [VT[t], ST] + CST)
                    i = nxt("sq", NSQ)
                    S.op("act", lambda e, i=i, t=t, pv=pv: e.activation(
                        out=sq[i][:, :], in_=vt[:, t, pv * 512:(pv + 1) * 512], func=AF.Square,
                        accum_out=st2[:, t, pv:pv + 1]), [VT[t], ST], [SQ[i], ST])
            S.op("dve", lambda e: e.tensor_reduce(out=stm[:, 0, :], in_=st1[:], axis=mybir.AxisListType.X, op=ALU.add),
                 [ST], [ST])
            S.op("dve", lambda e: e.tensor_reduce(out=stm[:, 1, :], in_=st2[:], axis=mybir.AxisListType.X, op=ALU.add),
                 [ST], [ST])
            S.op("dve", lambda e: e.tensor_scalar(out=stm[:, 0, :], in0=stm[:, 0, :], scalar1=1.0 / 2048, scalar2=None,
                                                  op0=ALU.mult), [ST], [ST])
            S.op("dve", lambda e: e.tensor_scalar(out=stm[:, 1, :], in0=stm[:, 1, :], scalar1=1.0 / 2048, scalar2=None,
                                                  op0=ALU.mult), [ST], [ST])
            S.op("dve", lambda e: e.tensor_tensor(out=stm[:, 3, :], in0=stm[:, 0, :], in1=stm[:, 0, :], op=ALU.mult),
                 [ST], [ST])
            S.op("dve", lambda e: e.tensor_tensor(out=stm[:, 1, :], in0=stm[:, 1, :], in1=stm[:, 3, :], op=ALU.subtract),
                 [ST], [ST])
            S.op("act", lambda e: e.activation(out=stm[:, 2, :], in_=stm[:, 1, :], func=AF.Sqrt, bias=cvals[:, 1:2],
                                               scale=1.0), [ST, CONST], [ST])
            S.op("dve", lambda e: e.reciprocal(out=stm[:, 2, :], in_=stm[:, 2, :]), [ST], [ST])
            for t in range(Tn):
                S.op("dve", lambda e, t=t: e.tensor_scalar(out=vt[:, t, :], in0=vt[:, t, :], scalar1=stm[:, 0, t:t + 1],
                                                           scalar2=stm[:, 2, t:t + 1], op0=ALU.subtract, op1=ALU.mult),
                     [VT[t], ST], [VT[t]])
            for fc in range(16):
                g = fc // 2
                bk, BK = bank()
                for t in range(Tn):
                    S.op("pe", lambda e, bk=bk, t=t, fc=fc, g=g: e.matmul(
                        bk[:, t * 128:(t + 1) * 128], lhsT=vt[:, t, fc * 128:(fc + 1) * 128], rhs=wsb[:, g, :],
                        start=True, stop=True), [VT[t], WSB], [BK], inc=(t == Tn - 1))
                i = nxt("tmp", NTMP)
                S.op("dve", lambda e, bk=bk, i=i, fc=fc, g=g: e.scalar_tensor_tensor(
                    out=tmp[i][:, :n].rearrange("p (t q) -> p t q", q=128),
                    in0=bk[:, :n].rearrange("p (t q) -> p t q", q=128), scalar=alng[:, j, fc:fc + 1],
                    in1=bsb[:, g, :].unsqueeze(1).broadcast_to([128, Tn, 128]), op0=ALU.mult, op1=ALU.add),
                    [BK, SMALL, WSF], [TMP[i]])
                S.op("dve", lambda e, i=i, fc=fc: e.tensor_tensor(out=hid[:, 16 + fc, :n], in0=tmp[i][:, :n],
                                                                  in1=hid[:, fc, :n], op=ALU.mult),
                     [TMP[i], HID[fc]], [HID[16 + fc]])
            proj_to_y(W_AOUT[j], awout_b[j], 16, lambda kk: hid[:, 16 + kk, :n], HID[16:], n)

        def load_ws(j):
            wsf = yT[:, 0:2, :].rearrange("p a (g q) -> p (a g) q", q=128)
            S.dma("sp", [lambda e: e.dma_start(out=wsf, in_=aws_d[j]),
                         lambda e: e.dma_start(out=bsb[:], in_=abs_d[j])], [], [WSF, Y[0], Y[1]], WSF)
            S.op("pool", lambda e: e.tensor_copy(out=wsb[:], in_=wsf), [WSF, Y[0], Y[1]], [WSB])

        def load_x(src_ap, t0, n, RD, xb):
            xT = xTs[xb]
            S.dma("sp", [lambda e: e.dma_start(out=xT[:, :, :n],
                                               in_=src_ap[:, t0:t0 + n].rearrange("(c p) t -> p c t", p=128))],
                  RD, Xs[xb], Xds[xb])

        def store_x(dst_ap, t0, n, WR):
            xb = CX["i"]
            xT = xTs[xb]
            S.dma("pool", [lambda e: e.dma_start(out=dst_ap[:, t0:t0 + n].rearrange("(c p) t -> p c t", p=128),
                                                 in_=xT[:, :, :n])], Xs[xb], WR, STOs[xb])

        PRE = {"loaded": False, "stats": None}

        def begin_group(src_ap, t0, n, RD):
            if PRE["loaded"]:
                CX["i"] = 1 - CX["i"]
                PRE["loaded"] = False
            else:
                load_x(src_ap, t0, n, RD, CX["i"])
                PRE["stats"] = None
            st_ = PRE["stats"]
            PRE["stats"] = None
            return st_

        def prefetch_load(src_ap, t0, n, RD):
            load_x(src_ap, t0, n, RD, 1 - CX["i"])
            PRE["loaded"] = True

        def prefetch_stats(n):
            PRE["stats"] = x_stats(n, 1 - CX["i"])

        groups = [(g, g * GT, GT, 0) for g in range(NG)] + [(NG, SEQ, CTX, 1)]

        load_ws(0)
        def sweep1_group(g, t0, n, col, nx):
            Tn = n // 128
            pre = begin_group(xT_d if col == 0 else ctxT_d, t0 if col == 0 else 0, n, [])
            pre_norm(0, col, n, 1, pre)
            gmlp(0, 0, col, n)
            post_norm(0, col, n, 1)
            mlp(0, col, n, hook=(lambda: prefetch_load(*nx[:4])) if nx else None)
            if debug:
                store_x(dbg_d[0], t0, n, [OUTD])
            pre_norm(1, col, n, 1)
            if nx and nx[4]:
                prefetch_stats(nx[2])
            for t in range(Tn):
                i = nxt("pqt", NPQ)
                for gr in range(4):
                    bk, BK = bank()
                    for mc in range(2):
                        S.op("pe", lambda e, bk=bk, gr=gr, mc=mc, t=t: e.matmul(
                            bk[:, :], lhsT=hT[:, gr * 2 + mc, t * 128:(t + 1) * 128], rhs=dftc[:, mc, :],
                            start=(mc == 0), stop=(mc == 1)), [H[gr * 2 + mc], TAB], [BK], inc=(mc == 1))
                    if gr % 2 == 0:
                        S.op("dve", lambda e, bk=bk, i=i, gr=gr: e.tensor_copy(out=pqt[i][:, gr * 512:(gr + 1) * 512],
                                                                              in_=bk[:, :]), [BK], [PQT[i]])
                    else:
                        S.op("act", lambda e, bk=bk, i=i, gr=gr: e.activation(out=pqt[i][:, gr * 512:(gr + 1) * 512],
                                                                             in_=bk[:, :], func=AF.Copy), [BK], [PQT[i]])
                ci = (t0 // 128 + t) if col == 0 else (32 + t)
                S.dma("pool", [lambda e, i=i, ci=ci: e.dma_start(out=pq_d[ci], in_=pqt[i][:, :])], [PQT[i]], [PQD[g]],
                      PQS[i])
            store_x(xs_d, t0, n, [XS[g]])

        for gi_, grp in enumerate(groups):
            if gi_ + 1 < len(groups):
                g2, t2, n2, c2 = groups[gi_ + 1]
                nx = (xT_d if c2 == 0 else ctxT_d, t2 if c2 == 0 else 0, n2, [], True)
            else:
                nx = (xs_d, 0, GT, [XS[0]], False)
            sweep1_group(*grp, nx)

        for l_ in range(4):
            need_mods(l_, 5)
        S.op("dve", lambda e: e.tensor_scalar(out=qkg[:, 0:2], in0=qkg[:, 0:2], scalar1=128.0 ** -0.5, scalar2=None,
                                              op0=ALU.mult), [SMALL], [SMALL])
        def sweep2_group(g, t0, n, col, nx):
            Tn = n // 128
            begin_group(xs_d, t0, n, [XS[g]])
            if col == 0:
                for ncn in range(32):
                    b = nxt("cst", NCS)
                    S.dma("sp", [lambda e, b=b, ncn=ncn: e.dma_start(out=cst[b], in_=dftl_d[g, ncn])], [],
                          [CST[b]] + VT, CST[b])
                    q_ = nxt("pqt", NPQ)
                    S.dma("sp", [lambda e, q_=q_, ncn=ncn: e.dma_start(out=pqt[q_][:, :], in_=pq_d[ncn])],
                          PQD[:NG], [PQT[q_]], PQT[q_])
                    for jc in range(8):
                        gr, half = jc // 2, jc % 2
                        S.op("pe", lambda e, jc=jc, gr=gr, half=half, q_=q_, b=b, ncn=ncn: e.matmul(
                            ps[jc][:, :], lhsT=pqt[q_][:, gr * 512 + half * 128: gr * 512 + half * 128 + 128],
                            rhs=cst[b][:, 0, :], start=(ncn == 0), stop=False), [PQT[q_], CST[b]], [PS[jc]], inc=False)
                        S.op("pe", lambda e, jc=jc, gr=gr, half=half, q_=q_, b=b, ncn=ncn: e.matmul(
                            ps[jc][:, :], lhsT=pqt[q_][:, gr * 512 + 256 + half * 128: gr * 512 + 256 + half * 128 + 128],
                            rhs=cst[b][:, 1, :], start=False, stop=(ncn == 31)), [PQT[q_], CST[b]], [PS[jc]],
                            inc=(jc == 7))
            else:
                for ncn in range(2):
                    q_ = nxt("pqt", NPQ)
                    S.dma("sp", [lambda e, q_=q_, ncn=ncn: e.dma_start(out=pqt[q_][:, :], in_=pq_d[32 + ncn])],
                          [PQD[NG]], [PQT[q_]], PQT[q_])
                    for jc in range(8):
                        gr, half = jc // 2, jc % 2
                        S.op("pe", lambda e, jc=jc, gr=gr, half=half, q_=q_, ncn=ncn: e.matmul(
                            ps[jc][:, :n], lhsT=pqt[q_][:, gr * 512 + half * 128: gr * 512 + half * 128 + 128],
                            rhs=dctx[:, ncn, 0, :], start=(ncn == 0), stop=False), [PQT[q_], TAB], [PS[jc]], inc=False)
                        S.op("pe", lambda e, jc=jc, gr=gr, half=half, q_=q_, ncn=ncn: e.matmul(
                            ps[jc][:, :n], lhsT=pqt[q_][:, gr * 512 + 256 + half * 128: gr * 512 + 256 + half * 128 + 128],
                            rhs=dctx[:, ncn, 1, :], start=False, stop=(ncn == 1)), [PQT[q_], TAB], [PS[jc]],
                            inc=(jc == 7))
            for jc in range(8):
                if jc % 2 == 0:
                    S.op("dve", lambda e, jc=jc: e.tensor_copy(out=hid[:, jc, :n], in_=ps[jc][:, :n]), [PS[jc]], [HID[jc]])
                else:
                    S.op("act", lambda e, jc=jc: e.activation(out=hid[:, jc, :n], in_=ps[jc][:, :n], func=AF.Copy),
                         [PS[jc]], [HID[jc]])
            rr["ps"] = 0
            proj_to_y(W_BOUT, bwout_b, 8, lambda kk: hid[:, kk, :n], HID, n)
            post_norm(1, col, n, 1)
            mlp(1, col, n, hook=(lambda: prefetch_load(*nx[:4])) if nx else None)
            if debug:
                store_x(dbg_d[1], t0, n, [OUTD])
            pre_norm(2, col, n, 1)
            if col == 0:
                S.dma("sp", [lambda e: e.dma_start(out=ropet[:, 0, :], in_=rope_d[0, :, t0:t0 + GT]),
                             lambda e: e.dma_start(out=ropet[:, 1, :], in_=rope_d[1, :, t0:t0 + GT])], [], [ROPE], ROPE)
            koff = (CTX + t0) if col == 0 else 0

            def head(pa, PA, ca, pb, PB, cb, kq, out_ap, OUT):
                bka, BKA = bank()
                for kc in range(8):
                    S.op("pe", lambda e, kc=kc: e.matmul(bka[:, :n], lhsT=pa[:, kc, ca:ca + 128], rhs=hT[:, kc, :n],
                                                         start=(kc == 0), stop=(kc == 7)), [PA, H[kc]], [BKA], inc=(kc == 7))
                if col == 0:
                    bkb, BKB = bank()
                    for kc in range(8):
                        S.op("pe", lambda e, kc=kc: e.matmul(bkb[:, :n], lhsT=pb[:, kc, cb:cb + 128], rhs=hT[:, kc, :n],
                                                             start=(kc == 0), stop=(kc == 7)), [PB, H[kc]], [BKB],
                             inc=(kc == 7))
                i = nxt("sq", NSQ)
                S.op("act", lambda e, i=i: e.activation(out=sq[i][:, :n], in_=bka[:, :n], func=AF.Square), [BKA], [SQ[i]])
                bkc, BKC = bank()
                S.op("pe", lambda e, i=i: e.matmul(bkc[:, :n], lhsT=ones[:, :], rhs=sq[i][:, :n], start=True, stop=True),
                     [SQ[i], CONST], [BKC], inc=True)
                r = nxt("rst", 2)
                S.op("act", lambda e, r=r: e.activation(out=rst[r][:, :n], in_=bkc[:, :n], func=AF.Sqrt,
                                                        bias=cvals[:, 0:1], scale=1.0 / 128), [BKC, CONST], [RST[r]])
                S.op("dve", lambda e, r=r: e.reciprocal(out=rst[r][:, :n], in_=rst[r][:, :n]), [RST[r]], [RST[r]])
                i1 = nxt("tmp", NTMP)
                if col == 0:
                    S.op("dve", lambda e, i1=i1: e.scalar_tensor_tensor(
                        out=tmp[i1][:, :n], in0=bka[:, :n], scalar=qkg[:, kq:kq + 1], in1=ropet[:, 0, :n],
                        op0=ALU.mult, op1=ALU.mult), [BKA, ROPE, SMALL], [TMP[i1]])
                    i2 = nxt("tmp", NTMP)
                    S.op("dve", lambda e, i2=i2: e.scalar_tensor_tensor(
                        out=tmp[i2][:, :n], in0=bkb[:, :n], scalar=qkg[:, kq + 1:kq + 2], in1=ropet[:, 1, :n],
                        op0=ALU.mult, op1=ALU.mult), [BKB, ROPE, SMALL], [TMP[i2]])
                    S.op("dve", lambda e, i1=i1, i2=i2: e.tensor_tensor(out=tmp[i1][:, :n], in0=tmp[i1][:, :n],
                                                                       in1=tmp[i2][:, :n], op=ALU.add),
                         [TMP[i1], TMP[i2]], [TMP[i1]])
                else:
                    S.op("dve", lambda e, i1=i1: e.tensor_scalar(out=tmp[i1][:, :n], in0=bka[:, :n], scalar1=qkg[:, kq:kq + 1],
                                                               scalar2=None, op0=ALU.mult), [BKA, SMALL], [TMP[i1]])
                S.op("dve", lambda e, i1=i1, r=r: e.tensor_tensor(out=out_ap, in0=tmp[i1][:, :n], in1=rst[r][:, :n],
                                                                op=ALU.mult), [TMP[i1], RST[r]], OUT)

            if col == 0:
                for half in range(2):
                    pa, PA = load_panel(W_QKV, qkv_b, 0, half * 512)
                    pb, PB = load_panel(W_QKV, qkv_b, 0, 1536 + half * 512)
                    for hh in range(4):
                        h = half * 4 + hh
                        head(pa, PA, hh * 128, pb, PB, hh * 128, 0, hid[:, h, :n], [HID[h]])
                S.dma("pool", [lambda e: e.dma_start(out=qT_dd[g], in_=hid[:, 0:8, :])], HID[0:8], [QTD[g]], STQ)
            pkv, PKV = load_panel(W_QKV, qkv_b, 0, 1024)
            if col == 0:
                pks, PKS = load_panel(W_QKV, qkv_b, 0, 2560, ncols=256)
            else:
                pks, PKS = pkv, PKV
            for kvh in range(2):
                head(pkv, PKV, kvh * 128, pks, PKS, kvh * 128, 2, KT[:, kvh, koff:koff + n], [KTt])
            for t in range(Tn):
                bk, BK = bank()
                for kc in range(8):
                    S.op("pe", lambda e, bk=bk, kc=kc, t=t: e.matmul(bk[:, 0:256], lhsT=hT[:, kc, t * 128:(t + 1) * 128],
                                                                     rhs=pkv[:, kc, 256:512], start=(kc == 0), stop=(kc == 7)),
                         [PKV, H[kc]], [BK], inc=(kc == 7))
                sc_i = koff // 128 + t
                S.op("act", lambda e, bk=bk, sc_i=sc_i: e.activation(out=Vs[:, sc_i, :], in_=bk[:, 0:256], func=AF.Copy),
                     [BK], [Vst])
            if col == 0:
                store_x(xs_d, t0, n, [XS[g]])

        for gi_, grp in enumerate(groups):
            if gi_ + 1 < len(groups):
                g2, t2, n2, c2 = groups[gi_ + 1]
                nx = (xs_d, t2, n2, [XS[g2]])
            else:
                nx = (xs_d, 0, GT, [XS[0]])
            sweep2_group(*grp, nx)

        load_ws(1)
        def load_q(gq):
            S.dma("sp", [lambda e: e.dma_start(out=hT[:, :, :], in_=qT_dd[gq])], [QTD[gq]], H, Hd)

        def sweep3_group(g, t0, n, col, nx):
            begin_group(xs_d, t0, n, [XS[g]])
            if not PRE.get("q"):
                load_q(g)
            PRE["q"] = False

            def attn_head(h):
                kvh = h // 4
                ob = 4 + 2 * (h % 2)
                bo, BO = ps[ob], PS[ob]
                bd, BD = ps[ob + 1], PS[ob + 1]
                sbk = [None] * NSC

                def s_mm(sc):
                    bi = sc % 4
                    bk, BK = ps[bi], PS[bi]
                    sbk[sc] = (bk, BK)
                    S.op("pe", lambda e, bk=bk, sc=sc: e.matmul(bk[:, :], lhsT=KT[:, kvh, sc * 128:(sc + 1) * 128],
                                                                rhs=hT[:, h, :], start=True, stop=True),
                         [KTt, H[h]], [BK], inc=True)
                s_mm(0)
                s_mm(1)
                for sc in range(NSC):
                    bk, BK = sbk[sc]
                    i = nxt("pt", NPT)
                    S.op("act", lambda e, bk=bk, i=i: e.activation(out=pt[i][:, :], in_=bk[:, :], func=AF.Exp,
                                                                   bias=cvals[:, 2:3], scale=1.0), [BK, CONST], [PT[i]])
                    if sc + 2 < NSC:
                        s_mm(sc + 2)
                    S.op("pe", lambda e, i=i, sc=sc: e.matmul(bo[:, :], lhsT=Vs[:, sc, kvh * 128:(kvh + 1) * 128],
                                                              rhs=pt[i][:, :], start=(sc == 0), stop=(sc == NSC - 1)),
                         [Vst, PT[i]], [BO], inc=False)
                    S.op("pe", lambda e, i=i, sc=sc: e.matmul(bd[:, :], lhsT=ones[:, :], rhs=pt[i][:, :],
                                                              start=(sc == 0), stop=(sc == NSC - 1)),
                         [CONST, PT[i]], [BD], inc=True)
                i = nxt("tmp", NTMP)
                S.op("dve", lambda e, i=i: e.reciprocal(out=tmp[i][:, :], in_=bd[:, :]), [BD], [TMP[i]])
                S.op("dve", lambda e, i=i, h=h: e.tensor_tensor(out=hid[:, h, :], in0=bo[:, :], in1=tmp[i][:, :],
                                                              op=ALU.mult), [BO, TMP[i]], [HID[h]])
            for h in range(8):
                attn_head(h)
            proj_to_y(W_WO, wo_b, 8, lambda kk: hid[:, kk, :n], HID, n)
            post_norm(2, 0, n, 1)
            mlp(2, 0, n)
            if debug:
                store_x(dbg_d[2], t0, n, [OUTD])
            pre_norm(3, 0, n, 1)
            gmlp(3, 1, 0, n)
            post_norm(3, 0, n, 1)

            def hook3():
                prefetch_load(*nx[:4])
                load_q(nx[4])
                PRE["q"] = True
            mlp(3, 0, n, hook=hook3 if nx else None)
            store_x(out_d, t0, n, [OUTD])

        for gi_, grp in enumerate(groups[:NG]):
            if gi_ + 1 < NG:
                g2, t2, n2, c2 = groups[gi_ + 1]
                nx = (xs_d, t2, n2, [XS[g2]], g2)
            else:
                nx = None
            sweep3_group(*grp, nx)

        S.final_wait("sp")
        S.emit()
    return nc


def _const_tables():
    bf = ml_dtypes.bfloat16
    m = np.arange(256, dtype=np.float64)
    ang = 2 * np.pi * np.outer(m, m) / 256.0
    dc = np.concatenate([np.cos(ang), np.sin(ang)], axis=1) / 16.0
    dft_c = dc.reshape(2, 128, 512).transpose(1, 0, 2).astype(np.float32).astype(bf)
    nk = (np.outer(np.arange(SEQ, dtype=np.int64), np.arange(SEQ, dtype=np.int64)) % SEQ).astype(np.int32)
    cvec = (np.cos(2 * np.pi * np.arange(SEQ) / SEQ) / 64.0).astype(np.float32).astype(bf)
    svec = (-np.sin(2 * np.pi * np.arange(SEQ) / SEQ) / 64.0).astype(np.float32).astype(bf)
    ct = cvec[nk].reshape(32, 128, 8, 512)
    stt = svec[nk].reshape(32, 128, 8, 512)
    dft_l = np.ascontiguousarray(np.stack([ct, stt], axis=3).transpose(2, 0, 1, 3, 4))
    n = np.arange(256, dtype=np.float64)
    ac = 2 * np.pi * np.outer(n, n) / 256.0
    dctx = np.stack([np.cos(ac), -np.sin(ac)], axis=1) / 16.0
    dctx = dctx.reshape(2, 128, 2, 256).transpose(1, 0, 2, 3)
    t = np.arange(SEQ)
    row = (t // 64).astype(np.float32)
    colp = (t % 64).astype(np.float32)
    inv = (np.float32(10000.0) ** (-np.arange(32, dtype=np.float32) / np.float32(32))).astype(np.float32)
    angr = np.concatenate([row[:, None] * inv, colp[:, None] * inv], axis=-1).astype(np.float32)
    cs, sn = np.cos(angr), np.sin(angr)
    cosT = np.concatenate([cs, cs], axis=1).T
    sinT = np.concatenate([-sn, sn], axis=1).T
    rope = np.stack([cosT, sinT], axis=0)
    return dict(dft_c=dft_c, dft_l=dft_l,
                dft_ctx=dctx.astype(np.float32).astype(bf), rope=np.ascontiguousarray(rope.astype(np.float32)))


def _prep_shared(c_ctx, ada_w, ada_b, norm_g, mlp_w1, mlp_w2, a_w_in, a_ln_g, a_w_s, a_b_s, a_w_out, b_w_out,
                 c_w_qkv, c_q_g, c_k_g, c_w_o):
    f = np.float32
    sw = (np.arange(128) + 64) % 128
    wq = c_w_qkv[0][:, :1024].reshape(D, 8, 128)
    wk = c_w_qkv[0][:, 1024:1280].reshape(D, 2, 128)
    qkv_ext = np.concatenate([c_w_qkv[0], wq[:, :, sw].reshape(D, 1024), wk[:, :, sw].reshape(D, 256)], axis=1)
    qkg = np.stack([c_q_g[0], c_q_g[0][sw], c_k_g[0], c_k_g[0][sw]], axis=1)
    d = dict(
        ada_w=np.ascontiguousarray(ada_w, dtype=f),
        ada_b=np.ascontiguousarray(ada_b.reshape(4, 48, 128).transpose(2, 0, 1), dtype=f),
        norm_g=np.ascontiguousarray(norm_g.reshape(4, 4, 8, 128).transpose(3, 0, 1, 2), dtype=f),
        mlp_w1=np.ascontiguousarray(mlp_w1, dtype=f), mlp_w2=np.ascontiguousarray(mlp_w2, dtype=f),
        a_w_in=np.ascontiguousarray(a_w_in, dtype=f),
        a_ln_g=np.ascontiguousarray(a_ln_g.reshape(2, 16, 128).transpose(2, 0, 1), dtype=f),
        a_w_s=np.ascontiguousarray(a_w_s.transpose(0, 3, 1, 2), dtype=f),
        a_b_s=np.ascontiguousarray(np.broadcast_to(a_b_s[:, None, :, :], (2, 128, 8, 128)), dtype=f),
        a_w_out=np.ascontiguousarray(a_w_out, dtype=f), b_w_out=np.ascontiguousarray(b_w_out[0], dtype=f),
        c_w_qkv=np.ascontiguousarray(qkv_ext, dtype=f), c_qk_g=np.ascontiguousarray(qkg, dtype=f),
        c_w_o=np.ascontiguousarray(c_w_o[0], dtype=f),
    )
    d.update(_const_tables())
    return d


def _core_inputs(shared, x_b, c_b, ctx_b, c_ctx):
    m = dict(shared)
    m["xT"] = np.ascontiguousarray(x_b.T, dtype=np.float32)
    m["ctxT"] = np.ascontiguousarray(ctx_b.T, dtype=np.float32)
    m["cc"] = np.ascontiguousarray(np.stack([c_b.reshape(8, 128).T, c_ctx.reshape(8, 128).T], axis=2), dtype=np.float32)
    return m


def kernel(x, c, ctx, c_ctx, ada_w, ada_b, norm_g, mlp_w1, mlp_w2, a_w_in, a_ln_g, a_w_s, a_b_s, a_w_out,
           b_w_out, c_w_qkv, c_q_g, c_k_g, c_w_o):
    args = [np.asarray(a) for a in (c_ctx, ada_w, ada_b, norm_g, mlp_w1, mlp_w2, a_w_in, a_ln_g, a_w_s, a_b_s, a_w_out,
                                    b_w_out, c_w_qkv, c_q_g, c_k_g, c_w_o)]
    x = np.asarray(x); c = np.asarray(c); ctx = np.asarray(ctx)
    shared = _prep_shared(*args)
    nc = build_program(debug=False)
    nb = x.shape[0]
    in_maps = [_core_inputs(shared, x[b], c[b], ctx[b], args[0]) for b in range(nb)]
    res = run_bass_kernel_spmd(nc, in_maps, core_ids=list(range(nb)))
    out = np.stack([np.asarray(r["outT"]).T for r in res.results], axis=0)
    return np.ascontiguousarray(out, dtype=np.float32)
```
